# Optimizing a Trainium2 kernel written in Bass

```python
import math
import jax, jax.numpy as jnp
from jax import lax
import numpy as np

D_MODEL = 1024
BATCH = 16
SEQ = 2048
DEPTH = 4
DEC_BATCH = 8
DEC_SEQ = 16
PAST_LEN = 1024

CHUNK = 64
Q_BLOCK = 128
EPS = 1e-6
D_CONV = 512
CONV_K = 31
N_HEADS = 8
Q_LORA = 384
KV_LORA = 256
QK_NOPE = 64
QK_ROPE = 32
V_DIM = 64
ROPE_BASE = 10000.0
ATTN_SCALE = (QK_NOPE + QK_ROPE) ** -0.5
D_INNER = 1024
SSM_HEAD_DIM = 64
SSM_HEADS = D_INNER // SSM_HEAD_DIM
SSM_GROUPS = 4
D_STATE = 128
SSM_CONV_K = 4
XBC_DIM = D_INNER + 2 * SSM_GROUPS * D_STATE
SSD_CHUNK = CHUNK
N_BRANCH = 3
ATTN_WIDTH = N_HEADS * V_DIM
MIX_WIDTH = D_CONV + ATTN_WIDTH + D_INNER
D_FF = -(-8 * D_MODEL // (3 * 256)) * 256
IN_SIZES = (2 * D_CONV, Q_LORA, KV_LORA, QK_ROPE, D_INNER, XBC_DIM, SSM_HEADS, N_BRANCH * D_MODEL)
IN_WIDTH = sum(IN_SIZES)
IN_SPLITS = tuple(sum(IN_SIZES[:i + 1]) for i in range(len(IN_SIZES) - 1))

kernel_name = 'hybrid_streaming_encoder_step'


def rmsnorm(x, g):
    x32 = x.astype(jnp.float32)
    y = x32 * lax.rsqrt(jnp.mean(x32 * x32, axis=-1, keepdims=True) + EPS)
    return (y * g.astype(jnp.float32)).astype(x.dtype)


def layernorm(x, g, b):
    x32 = x.astype(jnp.float32)
    xc = x32 - jnp.mean(x32, axis=-1, keepdims=True)
    y = xc * lax.rsqrt(jnp.mean(xc * xc, axis=-1, keepdims=True) + EPS)
    return (y * g.astype(jnp.float32) + b.astype(jnp.float32)).astype(x.dtype)


def causal_dwconv(x_pad, w, b):
    y = lax.conv_general_dilated(x_pad, w[:, None, :].astype(x_pad.dtype), window_strides=(1,),
                                 padding='VALID', dimension_numbers=('NWC', 'WIO', 'NWC'),
                                 feature_group_count=x_pad.shape[-1])
    return y + b.astype(y.dtype)


def rope(x, pos):
    half = x.shape[-1] // 2
    inv_freq = ROPE_BASE ** (-jnp.arange(half, dtype=jnp.float32) / half)
    ang = pos.astype(jnp.float32)[:, None] * inv_freq[None, :]
    cos = jnp.cos(ang)[None, :, None, :]
    sin = jnp.sin(ang)[None, :, None, :]
    x32 = x.astype(jnp.float32)
    x1, x2 = x32[..., :half], x32[..., half:]
    return jnp.concatenate([x1 * cos - x2 * sin, x2 * cos + x1 * sin], axis=-1).astype(x.dtype)


def mla_attend(q_nope, q_rope, k_nope, k_rope, v, q_pos, k_pos):
    s = jnp.einsum('bqhd,bkhd->bhqk', q_nope, k_nope) + jnp.einsum('bqhr,bkr->bhqk', q_rope, k_rope)
    s = s.astype(jnp.float32) * ATTN_SCALE
    visible = (k_pos[None, :] // CHUNK) <= (q_pos[:, None] // CHUNK)
    p = jax.nn.softmax(jnp.where(visible, s, -jnp.inf), axis=-1).astype(v.dtype)
    return jnp.einsum('bhqk,bkhd->bqhd', p, v)


def segsum(a):
    n = a.shape[-1]
    idx = jnp.arange(n)
    cs = jnp.cumsum(jnp.where(idx[:, None] > idx[None, :], a[..., :, None], 0.0), axis=-2)
    return jnp.where(idx[:, None] >= idx[None, :], cs, -jnp.inf)


def ssd(x, dt, a, bm, cm, h0, chunk):
    bsz, T, H, P = x.shape
    nc = T // chunk
    f32 = jnp.float32
    xdt = (x.astype(f32) * dt[..., None]).reshape(bsz, nc, chunk, H, P)
    bm = bm.astype(f32).reshape(bsz, nc, chunk, H, D_STATE)
    cm = cm.astype(f32).reshape(bsz, nc, chunk, H, D_STATE)
    da = jnp.moveaxis((dt * a).reshape(bsz, nc, chunk, H), 3, 1)
    da_cs = jnp.cumsum(da, axis=-1)
    decay_in = jnp.exp(segsum(da))
    y_diag = jnp.einsum('bclhn,bcshn,bhcls,bcshp->bclhp', cm, bm, decay_in, xdt)
    decay_to_end = jnp.exp(da_cs[..., -1:] - da_cs)
    states = jnp.einsum('bclhn,bhcl,bclhp->bchpn', bm, decay_to_end, xdt)
    states = jnp.concatenate([h0.astype(f32)[:, None], states], axis=1)
    decay_chunk = jnp.exp(segsum(jnp.pad(da_cs[..., -1], ((0, 0), (0, 0), (1, 0)))))
    states = jnp.einsum('bhzc,bchpn->bzhpn', decay_chunk, states)
    y_off = jnp.einsum('bclhn,bchpn,bhcl->bclhp', cm, states[:, :-1], jnp.exp(da_cs))
    y = (y_diag + y_off).reshape(bsz, T, H, P)
    return y.astype(x.dtype), states[:, -1].astype(h0.dtype)


def trunk_layer(x, past_lat, past_kr, conv_buf, ssm_buf, ssm_h0,
                g_pre_mix, g_post_mix, g_pre_ffn, g_post_ffn, w_in,
                conv_w, conv_b, conv_ln_g, conv_ln_b, g_q, w_uq, g_kv, w_ukv,
                ssm_conv_w, ssm_conv_b, dt_bias, a_log, d_skip, g_ssm,
                w_mix_out, w_out, w_gate_up, w_down):
    bsz, T, _ = x.shape
    n_past = past_lat.shape[1]
    pos = n_past + jnp.arange(T)
    h = rmsnorm(x, g_pre_mix)
    u = h @ w_in
    u_glu, u_q, u_kv, u_kr, u_z, u_xbc, u_dt, u_gate = jnp.split(u, IN_SPLITS, axis=-1)

    a = u_glu[..., :D_CONV] * jax.nn.sigmoid(u_glu[..., D_CONV:])
    a_pad = jnp.concatenate([conv_buf, a], axis=1)
    c = jax.nn.silu(layernorm(causal_dwconv(a_pad, conv_w, conv_b), conv_ln_g, conv_ln_b))
    o_conv = c @ w_mix_out[:D_CONV]

    q = (rmsnorm(u_q, g_q) @ w_uq).reshape(bsz, T, N_HEADS, QK_NOPE + QK_ROPE)
    q_nope, q_rope = q[..., :QK_NOPE], rope(q[..., QK_NOPE:], pos)
    lat = rmsnorm(u_kv, g_kv)
    kr = rope(u_kr[:, :, None, :], pos)[:, :, 0, :]
    lat_all = jnp.concatenate([past_lat, lat], axis=1)
    kr_all = jnp.concatenate([past_kr, kr], axis=1)
    k_pos = jnp.arange(n_past + T)
    kv = (lat_all @ w_ukv).reshape(bsz, n_past + T, N_HEADS, QK_NOPE + V_DIM)
    k_nope, v = kv[..., :QK_NOPE], kv[..., QK_NOPE:]
    blk = min(Q_BLOCK, T)
    nb = T // blk

    def to_blocks(t):
        return jnp.moveaxis(t.reshape(bsz, nb, blk, *t.shape[2:]), 1, 0)

    o = lax.map(lambda qs: mla_attend(qs[0], qs[1], k_nope, kr_all, v, qs[2], k_pos),
                (to_blocks(q_nope), to_blocks(q_rope), pos.reshape(nb, blk)))
    o = jnp.moveaxis(o, 0, 1).reshape(bsz, T, ATTN_WIDTH)
    o_attn = o @ w_mix_out[D_CONV:D_CONV + ATTN_WIDTH]

    xbc_pad = jnp.concatenate([ssm_buf, u_xbc], axis=1)
    xbc = jax.nn.silu(causal_dwconv(xbc_pad, ssm_conv_w, ssm_conv_b))
    xs, bm, cm = jnp.split(xbc, (D_INNER, D_INNER + SSM_GROUPS * D_STATE), axis=-1)
    xs = xs.reshape(bsz, T, SSM_HEADS, SSM_HEAD_DIM)
    rep = SSM_HEADS // SSM_GROUPS
    bm = jnp.repeat(bm.reshape(bsz, T, SSM_GROUPS, D_STATE), rep, axis=2)
    cm = jnp.repeat(cm.reshape(bsz, T, SSM_GROUPS, D_STATE), rep, axis=2)
    dt = jax.nn.softplus((u_dt + dt_bias).astype(jnp.float32))
    a_neg = -jnp.exp(a_log.astype(jnp.float32))
    y, h_new = ssd(xs, dt, a_neg, bm, cm, ssm_h0, min(SSD_CHUNK, T))
    y = (y + xs * d_skip[:, None]).reshape(bsz, T, D_INNER) * jax.nn.silu(u_z)
    y = rmsnorm(y.reshape(bsz, T, SSM_GROUPS, D_INNER // SSM_GROUPS),
                g_ssm.reshape(SSM_GROUPS, D_INNER // SSM_GROUPS)).reshape(bsz, T, D_INNER)
    o_ssm = y @ w_mix_out[D_CONV + ATTN_WIDTH:]

    gates = jax.nn.sigmoid(u_gate).reshape(bsz, T, N_BRANCH, D_MODEL)
    merged = gates[..., 0, :] * o_conv + gates[..., 1, :] * o_attn + gates[..., 2, :] * o_ssm
    x = x + rmsnorm(merged @ w_out, g_post_mix)
    f = rmsnorm(x, g_pre_ffn) @ w_gate_up
    f = (jax.nn.silu(f[..., :D_FF]) * f[..., D_FF:]) @ w_down
    x = x + rmsnorm(f, g_post_ffn)
    return x, lat, kr, a_pad[:, -(CONV_K - 1):], xbc_pad[:, -(SSM_CONV_K - 1):], h_new


def setup_inputs(seed: int = 0) -> dict:
    key = jax.random.key(seed)
    k = jax.random.split(key, 32)
    f32 = jnp.float32

    def nrm(kk, shape, scale):
        return scale * jax.random.normal(kk, shape, f32)

    def gain(kk, shape):
        return 1.0 + 0.05 * jax.random.normal(kk, shape, f32)

    dt0 = jnp.exp(jax.random.uniform(k[20], (DEPTH, SSM_HEADS), f32, math.log(1e-3), math.log(1e-1)))
    return {
        'x_prompt': nrm(k[0], (BATCH, SEQ, D_MODEL), 1.0),
        'x_sample': nrm(k[1], (DEC_BATCH, DEC_SEQ, D_MODEL), 1.0),
        'cache_mla_latent': nrm(k[2], (DEPTH, DEC_BATCH, PAST_LEN, KV_LORA), 1.0),
        'cache_mla_rope': nrm(k[3], (DEPTH, DEC_BATCH, PAST_LEN, QK_ROPE), 1.0),
        'state_conv': nrm(k[4], (DEPTH, DEC_BATCH, CONV_K - 1, D_CONV), 0.5),
        'state_ssm_conv': nrm(k[5], (DEPTH, DEC_BATCH, SSM_CONV_K - 1, XBC_DIM), 1.0),
        'state_ssm': nrm(k[6], (DEPTH, DEC_BATCH, SSM_HEADS, SSM_HEAD_DIM, D_STATE), 0.1),
        'g_pre_mix': gain(k[7], (DEPTH, D_MODEL)),
        'g_post_mix': gain(k[8], (DEPTH, D_MODEL)),
        'g_pre_ffn': gain(k[9], (DEPTH, D_MODEL)),
        'g_post_ffn': gain(k[10], (DEPTH, D_MODEL)),
        'w_in': nrm(k[11], (DEPTH, D_MODEL, IN_WIDTH), D_MODEL ** -0.5),
        'conv_w': nrm(k[12], (DEPTH, CONV_K, D_CONV), CONV_K ** -0.5),
        'conv_b': nrm(k[13], (DEPTH, D_CONV), 0.02),
        'conv_ln_g': gain(k[14], (DEPTH, D_CONV)),
        'conv_ln_b': nrm(k[15], (DEPTH, D_CONV), 0.02),
        'g_q': gain(k[16], (DEPTH, Q_LORA)),
        'w_uq': nrm(k[17], (DEPTH, Q_LORA, N_HEADS * (QK_NOPE + QK_ROPE)), Q_LORA ** -0.5),
        'g_kv': gain(k[18], (DEPTH, KV_LORA)),
        'w_ukv': nrm(k[19], (DEPTH, KV_LORA, N_HEADS * (QK_NOPE + V_DIM)), KV_LORA ** -0.5),
        'ssm_conv_w': nrm(k[21], (DEPTH, SSM_CONV_K, XBC_DIM), SSM_CONV_K ** -0.5),
        'ssm_conv_b': nrm(k[22], (DEPTH, XBC_DIM), 0.02),
        'dt_bias': dt0 + jnp.log(-jnp.expm1(-dt0)),
        'a_log': jnp.log(jax.random.uniform(k[23], (DEPTH, SSM_HEADS), f32, 1.0, 16.0)),
        'd_skip': gain(k[24], (DEPTH, SSM_HEADS)),
        'g_ssm': gain(k[25], (DEPTH, D_INNER)),
        'w_mix_out': nrm(k[26], (DEPTH, MIX_WIDTH, D_MODEL), D_MODEL ** -0.5),
        'w_out': nrm(k[27], (DEPTH, D_MODEL, D_MODEL), D_MODEL ** -0.5),
        'w_gate_up': nrm(k[28], (DEPTH, D_MODEL, 2 * D_FF), D_MODEL ** -0.5),
        'w_down': nrm(k[29], (DEPTH, D_FF, D_MODEL), D_FF ** -0.5),
    }


def reference(x_prompt, x_sample, cache_mla_latent, cache_mla_rope, state_conv, state_ssm_conv, state_ssm,
              g_pre_mix, g_post_mix, g_pre_ffn, g_post_ffn, w_in, conv_w, conv_b, conv_ln_g, conv_ln_b,
              g_q, w_uq, g_kv, w_ukv, ssm_conv_w, ssm_conv_b, dt_bias, a_log, d_skip, g_ssm,
              w_mix_out, w_out, w_gate_up, w_down):
    bp, dtype = x_prompt.shape[0], x_prompt.dtype
    no_lat = jnp.zeros((bp, 0, KV_LORA), dtype)
    no_kr = jnp.zeros((bp, 0, QK_ROPE), dtype)
    zero_conv = jnp.zeros((bp, CONV_K - 1, D_CONV), dtype)
    zero_ssm_conv = jnp.zeros((bp, SSM_CONV_K - 1, XBC_DIM), dtype)
    zero_ssm = jnp.zeros((bp, SSM_HEADS, SSM_HEAD_DIM, D_STATE), dtype)
    yp, ys = x_prompt, x_sample
    p_lat, p_kr, p_conv, p_sconv, p_ssm = [], [], [], [], []
    s_lat, s_kr, s_conv, s_sconv, s_ssm = [], [], [], [], []
    for l in range(DEPTH):
        lw = (g_pre_mix[l], g_post_mix[l], g_pre_ffn[l], g_post_ffn[l], w_in[l],
              conv_w[l], conv_b[l], conv_ln_g[l], conv_ln_b[l], g_q[l], w_uq[l], g_kv[l], w_ukv[l],
              ssm_conv_w[l], ssm_conv_b[l], dt_bias[l], a_log[l], d_skip[l], g_ssm[l],
              w_mix_out[l], w_out[l], w_gate_up[l], w_down[l])
        yp, lat, kr, cb, sb, hs = trunk_layer(yp, no_lat, no_kr, zero_conv, zero_ssm_conv, zero_ssm, *lw)
        p_lat.append(lat)
        p_kr.append(kr)
        p_conv.append(cb)
        p_sconv.append(sb)
        p_ssm.append(hs)
        ys, lat, kr, cb, sb, hs = trunk_layer(ys, cache_mla_latent[l], cache_mla_rope[l], state_conv[l],
                                              state_ssm_conv[l], state_ssm[l], *lw)
        s_lat.append(lat)
        s_kr.append(kr)
        s_conv.append(cb)
        s_sconv.append(sb)
        s_ssm.append(hs)
    return (yp, ys, jnp.stack(p_lat), jnp.stack(p_kr), jnp.stack(p_conv), jnp.stack(p_sconv), jnp.stack(p_ssm),
            jnp.stack(s_lat), jnp.stack(s_kr), jnp.stack(s_conv), jnp.stack(s_sconv), jnp.stack(s_ssm))
```

```python
import math
import os
from contextlib import ExitStack

import numpy as np
import concourse.bass as bass
import concourse.mybir as mybir
from concourse.bass_utils import run_bass_kernel_spmd

F32 = mybir.dt.float32
BF16 = mybir.dt.bfloat16
AF = mybir.ActivationFunctionType
ALU = mybir.AluOpType

D = 1024
KC = 8
D_CONV = 512
CONV_K = 31
NH = 8
Q_LORA = 384
KV_LORA = 256
QK_NOPE = 64
QK_ROPE = 32
V_DIM = 64
QD = QK_NOPE + QK_ROPE
ATTN_SCALE = QD ** -0.5
D_INNER = 1024
HP = 64
SH = 16
SG = 4
NS = 128
XBC = 2048
D_FF = 2816
FC = 22
EPS = 1e-6
PAST = 1024
DEC_T = 16
IN_W = 7856
C_GLU, C_Q, C_KV, C_KR, C_Z, C_XBC, C_DT, C_GATE = 0, 1024, 1408, 1664, 1696, 2720, 4768, 4784

NSLOT = 4
SLOT_ELEMS = 4096


class Op:
    __slots__ = ("eng", "meth", "kw", "deps", "is_dma", "chan", "cnt", "signal", "sigcnt", "waits")

    def __init__(self, eng, meth, kw, deps, is_dma=False, chan=None, cnt=0):
        self.eng = eng
        self.meth = meth
        self.kw = kw
        self.deps = deps
        self.is_dma = is_dma
        self.chan = chan
        self.cnt = cnt
        self.signal = False
        self.sigcnt = 0
        self.waits = []


class Prog:
    ENGS = ("pe", "act", "dve", "pool", "sp")

    def __init__(self, nc, stack):
        self.nc = nc
        self.stack = stack
        self.ops = []
        self.st = {}
        self.chan_last = {}
        self.chan_cnt = {}
        self.n_sb = 0

    def sb(self, name, shape, dtype):
        return self.stack.enter_context(self.nc.sbuf_tensor(name, list(shape), dtype))

    def psum(self, name, shape, dtype):
        return self.stack.enter_context(self.nc.psum_tensor(name, list(shape), dtype))

    def _track(self, idx, eng, R, W, is_dma):
        deps = set()
        for k in R:
            s = self.st.get(k)
            if s is not None and s[0] is not None:
                deps.add(s[0])
            if s is not None and isinstance(k, str) and k.startswith("ps"):
                for e2, v in s[1]:
                    if e2 != eng:
                        deps.add(v)
        for k in W:
            s = self.st.get(k)
            if s is not None:
                if s[0] is not None:
                    deps.add(s[0])
                deps.update(v for _, v in s[1])
                deps.update(s[2])
        for k in R:
            s = self.st.setdefault(k, [None, [], []])
            if is_dma:
                s[2].append(idx)
            else:
                s[1].append((eng, idx))
        for k in W:
            self.st[k] = [idx, [], []]
        deps.discard(idx)
        return deps

    def op(self, eng, meth, R=(), W=(), **kw):
        idx = len(self.ops)
        deps = self._track(idx, eng, R, W, False)
        self.ops.append(Op(eng, meth, kw, deps))
        return idx

    def dma(self, q, out, in_, R, W, chan, **kw):
        idx = len(self.ops)
        deps = self._track(idx, q, R, W, True)
        if chan in self.chan_last:
            deps.add(self.chan_last[chan])
        self.chan_last[chan] = idx
        self.chan_cnt[chan] = self.chan_cnt.get(chan, 0) + 1
        kw = dict(kw)
        kw["out"] = out
        kw["in_"] = in_
        self.ops.append(Op(q, "dma_start", kw, deps, True, chan, self.chan_cnt[chan]))
        return idx

    def _dur(self, o):
        kw = o.kw
        try:
            if o.is_dma:
                return 0.2 if o.eng == "sp" else 1.2
            if o.eng == "pe":
                mv = kw["rhs"] if o.meth == "matmul" else kw["in_"]
                n = mv.free_size()
                f = 4.0 if mv.dtype == F32 else 1.0
                return (max(64, n) * f) / 2.0 + 12.0
            outap = kw.get("out", kw.get("ap"))
            fd = outap.free_size()
            if o.eng == "act":
                return (224 + fd) / 1.2
            if o.eng == "pool":
                return (150 + fd * 2.0) / 1.2
            f = 1.0
            ins = [kw[k] for k in ("in0", "in1", "in_") if k in kw and hasattr(kw[k], "dtype")]
            if ins and all(a.dtype == BF16 for a in ins) and outap.dtype == BF16:
                f = 0.5
            if o.meth == "memset":
                f = 0.5
            return (120 + fd * f) / 0.96
        except Exception:
            return 500.0

    def _dma_bytes(self, o):
        try:
            a = o.kw["in_"]
            n = 1
            for d in a.shape:
                n *= d
            return n * (4 if a.dtype == F32 else 2)
        except Exception:
            return 1 << 20

    def schedule(self):
        import heapq
        ops = self.ops
        n = len(ops)
        dur = [self._dur(o) for o in ops]
        tail = [0.0] * n
        for i, o in enumerate(ops):
            if o.is_dma:
                tail[i] = 2000.0 + self._dma_bytes(o) / 300.0
        keep = os.environ.get("SCHED_KEEP", "").split(",")
        lastop = {}
        kr_ = os.environ.get("KEEP_RANGE", "")
        klo, khi = (int(v) for v in kr_.split(":")) if kr_ else (0, 1 << 60)
        for i, o in enumerate(ops):
            if o.eng in keep and klo <= i < khi:
                if o.eng in lastop:
                    o.deps.add(lastop[o.eng])
                lastop[o.eng] = i
        succ = [[] for _ in range(n)]
        indeg = [0] * n
        for i, o in enumerate(ops):
            for j in o.deps:
                succ[j].append(i)
            indeg[i] = len(o.deps)
        bl = [0.0] * n
        for i in range(n - 1, -1, -1):
            m = 0.0
            for k in succ[i]:
                if bl[k] > m:
                    m = bl[k]
            bl[i] = m + dur[i] + tail[i]
        SYNC = 120.0
        ready = [0.0] * n
        rsrc = [-1] * n
        gaps = {}
        fin = [0.0] * n
        pend = {e: [] for e in self.ENGS}
        avail = {e: [] for e in self.ENGS}
        ft = {e: 0.0 for e in self.ENGS}
        for i in range(n):
            if indeg[i] == 0:
                heapq.heappush(pend[ops[i].eng], (0.0, i))
        dma_free = 0.0
        acls = [({"Exp": "E", "Ln": "E", "Sigmoid": "S", "Silu": "U"}.get(str(o.kw.get("func", "")).split(".")[-1])
                 if o.eng == "act" else None) for o in ops]
        act_cur = [None]
        order = {e: [] for e in self.ENGS}
        self.gorder = []
        done = 0
        while done < n:
            best_e = None
            best_t = None
            for e in self.ENGS:
                pe_, av_ = pend[e], avail[e]
                while pe_ and pe_[0][0] <= ft[e]:
                    r, i = heapq.heappop(pe_)
                    heapq.heappush(av_, (-bl[i], i))
                if av_:
                    t = ft[e]
                elif pe_:
                    t = pe_[0][0]
                else:
                    continue
                if best_t is None or t < best_t:
                    best_t = t
                    best_e = e
            e = best_e
            if avail[e]:
                if e == "act" and len(avail[e]) > 1:
                    cand = [heapq.heappop(avail[e]) for _ in range(min(6, len(avail[e])))]
                    pick = 0
                    if acls[cand[0][1]] not in (None, act_cur[0]):
                        for ci, (nb_, ii) in enumerate(cand):
                            if acls[ii] in (None, act_cur[0]) and -nb_ >= -cand[0][0] - 3000.0:
                                pick = ci
                                break
                    _, i = cand.pop(pick)
                    for c_ in cand:
                        heapq.heappush(avail[e], c_)
                else:
                    _, i = heapq.heappop(avail[e])
                start = ft[e]
            else:
                r, i = heapq.heappop(pend[e])
                start = r
            o = ops[i]
            if e == "act" and acls[i] is not None and acls[i] != act_cur[0]:
                act_cur[0] = acls[i]
                start += 1300.0
            if start > ft[e] + 1e-9 and rsrc[i] >= 0:
                b_ = ops[rsrc[i]]
                kk_ = (e, b_.eng, b_.meth, "dma" if b_.is_dma else str(b_.kw.get("func", "")).split(".")[-1])
                gaps[kk_] = gaps.get(kk_, 0.0) + (start - ft[e])
            ft[e] = start + dur[i]
            if o.is_dma:
                xs = max(ft[e], dma_free)
                xf = xs + self._dma_bytes(o) / 300.0
                dma_free = xf
                fin[i] = xf + 2000.0
            else:
                fin[i] = ft[e]
            order[e].append(i)
            self.gorder.append(i)
            done += 1
            for k in succ[i]:
                rt = fin[i] + (0.0 if (ops[k].eng == e and e == "pe" and not o.is_dma) else SYNC)
                if rt > ready[k]:
                    ready[k] = rt
                    rsrc[k] = i
                indeg[k] -= 1
                if indeg[k] == 0:
                    heapq.heappush(pend[ops[k].eng], (ready[k], k))
        self.est_ns = max(fin) if n else 0.0
        busy = {e: 0.0 for e in self.ENGS}
        cntm = {}
        for i, o in enumerate(ops):
            busy[o.eng] += dur[i]
            kk = (o.eng, o.meth)
            c_ = cntm.setdefault(kk, [0, 0.0])
            c_[0] += 1
            c_[1] += dur[i]
        self.est_info = (busy, dma_free, cntm)
        self.gaps = gaps
        return order

    def finalize(self, resched=True):
        nc = self.nc
        ops = self.ops
        if resched:
            order = self.schedule()
        else:
            order = {e: [i for i, o in enumerate(ops) if o.eng == e] for e in self.ENGS}
        cls_prev = None
        nsw = 0
        for i in order["act"]:
            f_ = str(ops[i].kw.get("func", "")).split(".")[-1]
            c_ = {"Exp": "E", "Ln": "E", "Sigmoid": "S", "Silu": "U"}.get(f_)
            if c_ and c_ != cls_prev:
                nsw += 1
                cls_prev = c_
        self.n_tblsw = nsw
        pos = [0] * len(ops)
        for e in self.ENGS:
            for p_, i in enumerate(order[e]):
                pos[i] = p_
        gorder = getattr(self, "gorder", None) if resched else None
        if not gorder:
            gorder = list(range(len(ops)))
        waited = {e: {} for e in self.ENGS}
        know = [None] * len(ops)
        nw = 0
        for i in gorder:
            o = ops[i]
            e = o.eng
            cur = waited[e]
            need = {}
            for j in o.deps:
                oj = ops[j]
                if oj.is_dma:
                    src = ("c", oj.chan)
                    val = oj.cnt
                else:
                    if oj.eng == "pe" and e == "pe" and not o.is_dma:
                        assert pos[j] < pos[i]
                        continue
                    src = oj.eng
                    val = pos[j]
                    if src == e:
                        assert pos[j] < pos[i]
                if src not in need or val > need[src][0]:
                    need[src] = (val, j)
            waits = []
            for src, (val, j) in sorted(need.items(), key=lambda t: -len(know[t[1][1]] or ())):
                if cur.get(src, -1) >= val:
                    continue
                waits.append((src, j))
                cur[src] = val
                kj = know[j]
                if kj:
                    for s2, v2 in kj.items():
                        if cur.get(s2, -1) < v2:
                            cur[s2] = v2
                if not ops[j].is_dma:
                    ops[j].signal = True
            o.waits = waits
            nw += len(waits)
            k_ = dict(cur)
            if not o.is_dma:
                k_[e] = max(k_.get(e, -1), pos[i])
            know[i] = k_
        self.n_waits = nw
        for e in self.ENGS:
            c = 0
            for i in order[e]:
                o = ops[i]
                if (not o.is_dma) and o.signal:
                    c += 1
                    o.sigcnt = c
        esem = {e: self.stack.enter_context(nc.semaphore("s_" + e)) for e in ("pe", "act", "dve", "pool")}
        csem = {c: self.stack.enter_context(nc.semaphore("c_%d" % i)) for i, c in enumerate(self.chan_cnt)}
        chan_final = dict(self.chan_cnt)
        block = self.stack.enter_context(nc.Block())

        def make_body(e):
            def body(eng):
                for i in order[e]:
                    o = ops[i]
                    for src, j in o.waits:
                        if isinstance(src, tuple):
                            eng.wait_ge(csem[src[1]], 16 * ops[j].cnt)
                        else:
                            eng.wait_ge(esem[src], ops[j].sigcnt)
                    ins = getattr(eng, o.meth)(**o.kw)
                    if o.is_dma:
                        ins.then_inc(csem[o.chan], 16)
                    elif o.signal:
                        ins.then_inc(esem[e], 1)
                if e == "sp":
                    for c, n in chan_final.items():
                        eng.wait_ge(csem[c], 16 * n)
            return body

        block.tensor(make_body("pe"))
        block.scalar(make_body("act"))
        block.vector(make_body("dve"))
        block.gpsimd(make_body("pool"))
        block.sync(make_body("sp"))


class Pool:
    def __init__(self, P, name, n, shape, dtype):
        self.free = []
        for i in range(n):
            t = P.sb("%s%d" % (name, i), shape, dtype)
            self.free.append((t, "%s%d" % (name, i)))
        self.n = n

    def get(self):
        assert self.free, "pool exhausted"
        return self.free.pop(0)

    def put(self, b):
        self.free.append(b)


def bc_ap(t, part, dims, offset=0):
    pstep = t[:].ap[0][0]
    return bass.AP(t, offset, [[pstep, part]] + [list(d) for d in dims])


class _Stop(Exception):
    pass


class Cfg:
    def __init__(self, nseq=2, seq=2048, depth=4, sample=True, tt=256, dbg=(), stop=99):
        self.nseq = nseq
        self.seq = seq
        self.depth = depth
        self.sample = sample
        self.tt = tt
        self.dbg = tuple(dbg)
        self.stop = stop


def host_consts(cfg):
    c = {}
    c["ident"] = np.eye(128, dtype=np.float32)
    k = np.arange(128)
    c["tri"] = (k[:, None] <= k[None, :]).astype(np.float32)
    c["lstrict"] = (k[:, None] > k[None, :]).astype(np.float32)
    half = QK_ROPE // 2
    inv_freq = (10000.0 ** (-np.arange(half, dtype=np.float32) / half)).astype(np.float32)
    pos = np.concatenate([np.arange(cfg.seq), PAST + np.arange(DEC_T)]).astype(np.float32)
    ang = pos[None, :] * inv_freq[:, None]
    cos = np.ones((QD, pos.shape[0]), np.float32)
    sin = np.zeros((QD, pos.shape[0]), np.float32)
    cos[64:80] = np.cos(ang)
    cos[80:96] = np.cos(ang)
    sin[64:80] = np.sin(ang)
    sin[80:96] = np.sin(ang)
    c["rcos"] = cos
    c["rsin"] = sin
    return c


WNAMES = ["g_pre_mix", "g_post_mix", "g_pre_ffn", "g_post_ffn", "w_in", "conv_w", "conv_b", "conv_ln_g",
          "conv_ln_b", "g_q", "w_uq", "g_kv", "w_ukv", "ssm_conv_w", "ssm_conv_b", "dt_bias", "a_log",
          "d_skip", "g_ssm", "w_mix_out", "w_out", "w_gate_up", "w_down"]


def build(cfg):
    nc = bass.Bass("TRN2", target_bir_lowering=False)
    L, NSEQ, SEQ, TT = cfg.depth, cfg.nseq, cfg.seq, cfg.tt
    NT = SEQ // TT

    def din(name, shape):
        return nc.dram_tensor(name, list(shape), F32, kind="ExternalInput").ap()

    def dout(name, shape):
        return nc.dram_tensor(name, list(shape), F32, kind="ExternalOutput").ap()

    I = {}
    I["x_prompt"] = din("x_prompt", [NSEQ, SEQ, D])
    I["x_sample"] = din("x_sample", [1, DEC_T, D])
    I["cache_mla_latent"] = din("cache_mla_latent", [L, 1, PAST, KV_LORA])
    I["cache_mla_rope"] = din("cache_mla_rope", [L, 1, PAST, QK_ROPE])
    I["state_conv"] = din("state_conv", [L, 1, CONV_K - 1, D_CONV])
    I["state_ssm_conv"] = din("state_ssm_conv", [L, 1, 3, XBC])
    I["state_ssm"] = din("state_ssm", [L, 1, SH, HP, NS])
    wshapes = {"g_pre_mix": [L, D], "g_post_mix": [L, D], "g_pre_ffn": [L, D], "g_post_ffn": [L, D],
               "w_in": [L, D, IN_W], "conv_w": [L, CONV_K, D_CONV], "conv_b": [L, D_CONV],
               "conv_ln_g": [L, D_CONV], "conv_ln_b": [L, D_CONV], "g_q": [L, Q_LORA],
               "w_uq": [L, Q_LORA, NH * QD], "g_kv": [L, KV_LORA], "w_ukv": [L, KV_LORA, NH * 128],
               "ssm_conv_w": [L, 4, XBC], "ssm_conv_b": [L, XBC], "dt_bias": [L, SH], "a_log": [L, SH],
               "d_skip": [L, SH], "g_ssm": [L, D_INNER], "w_mix_out": [L, 2048, D], "w_out": [L, D, D],
               "w_gate_up": [L, D, 2 * D_FF], "w_down": [L, D_FF, D]}
    for n in WNAMES:
        I[n] = din(n, wshapes[n])
    NPOS = SEQ + DEC_T
    I["c_ident"] = din("c_ident", [128, 128])
    I["c_tri"] = din("c_tri", [128, 128])
    I["c_lstrict"] = din("c_lstrict", [128, 128])
    I["c_rcos"] = din("c_rcos", [QD, NPOS])
    I["c_rsin"] = din("c_rsin", [QD, NPOS])

    O = {}
    O["y_prompt"] = dout("y_prompt", [NSEQ, SEQ, D])
    O["y_sample"] = dout("y_sample", [1, DEC_T, D])
    O["lat_p"] = dout("lat_p", [L, NSEQ, SEQ, KV_LORA])
    O["kr_p"] = dout("kr_p", [L, NSEQ, SEQ, QK_ROPE])
    O["conv_p"] = dout("conv_p", [L, NSEQ, CONV_K - 1, D_CONV])
    O["sconv_p"] = dout("sconv_p", [L, NSEQ, 3, XBC])
    O["ssm_p"] = dout("ssm_p", [L, NSEQ, SH, HP, NS])
    O["lat_s"] = dout("lat_s", [L, 1, DEC_T, KV_LORA])
    O["kr_s"] = dout("kr_s", [L, 1, DEC_T, QK_ROPE])
    O["conv_s"] = dout("conv_s", [L, 1, CONV_K - 1, D_CONV])
    O["sconv_s"] = dout("sconv_s", [L, 1, 3, XBC])
    O["ssm_s"] = dout("ssm_s", [L, 1, SH, HP, NS])
    DBG = {}
    for (nm, shp) in cfg.dbg:
        DBG[nm] = dout("dbg_" + nm, shp)

    xres = nc.dram_tensor("xres", [NSEQ + 1, 128, KC, SEQ], F32, kind="Internal").ap()
    dgd = nc.dram_tensor("wsc", [L, 64, 128, SLOT_ELEMS], BF16, kind="Internal").ap()

    stack = ExitStack()
    with stack:
        P = Prog(nc, stack)
        emit_program(P, cfg, I, O, DBG, xres, dgd)
        print('sbuf_free', nc.sbuf_bytes_remaining, flush=True)
        P.finalize(resched=not os.environ.get('NO_RESCHED'))
        print('tblsw', getattr(P, 'n_tblsw', -1), 'waits', getattr(P, 'n_waits', -1), 'ops', len(P.ops), 'est_ms', getattr(P, 'est_ns', 0) / 1e6, {k: round(v / 1e6, 2) for k, v in P.est_info[0].items()}, 'dma_ms', P.est_info[1] / 1e6, {k: (v[0], round(v[1] / 1e6, 2)) for k, v in P.est_info[2].items()}, flush=True)
        print('gaps(ms)', sorted([(k, round(v / 1e6, 2)) for k, v in P.gaps.items() if v > 2e5], key=lambda t: -t[1]), flush=True)
    return nc


def emit_program(P, cfg, I, O, DBG, xres, dgd):
    nc = P.nc
    L, NSEQ, SEQ, TT = cfg.depth, cfg.nseq, cfg.seq, cfg.tt
    NT = SEQ // TT
    NPOS = SEQ + DEC_T

    ident_f = P.sb("ident_f", [128, 128], F32)
    ident_b = P.sb("ident_b", [128, 128], BF16)
    ones_f = P.sb("ones_f", [128, 128], F32)
    ones_b = P.sb("ones_b", [128, 128], BF16)
    tri_f = P.sb("tri_f", [128, 128], F32)
    tri_b = P.sb("tri_b", [128, 128], BF16)
    lst_b = P.sb("lst_b", [128, 128], BF16)
    NPC = L * (4 * 8 + 3 * 4 + 3 + 2 + 16 + 8 + 124 + 64)
    params = P.sb("params", [128, NPC], F32)
    bcp = P.sb("bcp", [128, 3, L * SH], F32)
    slots = [P.sb("wslot%d" % i, [128, SLOT_ELEMS], BF16) for i in range(NSLOT)]
    wuq_rot = P.sb("wuq_rot", [128, 3, NH * QD], BF16)
    wkr_pad = P.sb("wkr_pad", [128, KC, QD], BF16)
    wkr_rot = P.sb("wkr_rot", [128, KC, QD], BF16)
    KMAX = max(SEQ, PAST + DEC_T)
    NKB = (KMAX + 127) // 128
    Kc = P.sb("Kc", [QD, NH, KMAX], BF16)
    Vc = P.sb("Vc", [128, NKB, NH, 66], BF16)
    state_f = P.sb("state_f", [128, 1024], F32)
    state_b = P.sb("state_b", [128, 1024], BF16)
    xbuf = [P.sb("xT%d" % i, [128, KC, TT], F32) for i in range(2)]
    rope_t = P.sb("rope_t", [QD, 2, TT], F32)
    a_buf = P.sb("a_buf", [128, 4, 30 + TT], BF16)
    xhist = P.sb("xhist", [128, 16, 3], BF16)
    rawb = [P.sb("rawb%d" % i, [128, 4, 3 + TT], BF16) for i in range(2)]
    Qp = P.sb("Qp", [QD, NH, TT], BF16)
    On = P.sb("On", [64, NH, TT], BF16)
    yT = P.sb("yT", [128, KC, TT], BF16)
    NPB = 3
    pbufs = [P.sb("pbuf%d" % i, [128, 2, TT], BF16) for i in range(NPB)]
    bpool = Pool(P, "bch", 34, [128, TT], BF16)
    fpool = Pool(P, "fch", 11, [128, TT], F32)
    R_t = P.sb("R_t", [128, SH, 128], BF16)
    eD_t = P.sb("eD_t", [128, SH, 128], BF16)
    Gm_t = P.sb("Gm_t", [128, SG, 128], BF16)
    xdt_t = P.sb("xdt_t", [128, SH, HP], BF16)
    xdte_t = P.sb("xdte_t", [128, SH, HP], BF16)
    xD_t = P.sb("xD_t", [128, 1024], BF16)
    Btm_t = P.sb("Btm_t", [128, 512], BF16)
    yt_t = P.sb("yt_t", [128, 1024], F32)
    ybf_t = P.sb("ybf_t", [128, 1024], BF16)
    sm = P.sb("ssd_small", [128, 8, SH], F32)
    tokout = P.sb("tokout", [128, 1024], F32)
    tokin = P.sb("tokin", [128, 1024], F32)
    latT = P.sb("latT", [128, 1, 256], F32)
    krT = P.sb("krT", [128, 1, 32], F32)
    qt1 = P.sb("qt1", [QD, TT], F32)
    qt2 = P.sb("qt2", [QD, TT], F32)
    krf = P.sb("krf", [QD, TT], F32)
    rden = P.sb("rden", [128, TT], F32)
    bcs = P.sb("bcs", [64, TT], F32)
    st30 = P.sb("st30", [32, 512], F32)
    kr96 = P.sb("kr96", [128, QD], F32)

    psf = [P.psum("psf%d" % i, [128, 512], F32) for i in range(8)]
    ps_rr = [0]

    def PS(excl=()):
        while True:
            i = ps_rr[0] % 8
            ps_rr[0] += 1
            if ("psf%d" % i) not in excl:
                return psf[i], "psf%d" % i
    psb_rr = [0]

    def PSB():
        t_, k_ = PS()
        return t_[:].bitcast(BF16), k_

    def mm(out, lhsT, rhs, start, stop, R, W):
        P.op("pe", "matmul", R=R, W=W, out=out, lhsT=lhsT, rhs=rhs, start=start, stop=stop)

    def tr(out, in_, ident, R, W):
        P.op("pe", "transpose", R=R, W=W, out=out, in_=in_, identity=ident)

    def act(out, in_, func, R, W, **kw):
        P.op("act", "activation", R=R, W=W, out=out, in_=in_, func=func, **kw)

    def tt(eng, out, in0, in1, op, R, W):
        P.op(eng, "tensor_tensor", R=R, W=W, out=out, in0=in0, in1=in1, op=op)

    def ts(eng, out, in0, s1, s2, op0, op1, R, W):
        if op1 is None:
            P.op(eng, "tensor_scalar", R=R, W=W, out=out, in0=in0, scalar1=s1, scalar2=None, op0=op0)
        else:
            P.op(eng, "tensor_scalar", R=R, W=W, out=out, in0=in0, scalar1=s1, scalar2=s2, op0=op0, op1=op1)

    def stt(eng, out, in0, scalar, in1, op0, op1, R, W):
        P.op(eng, "scalar_tensor_tensor", R=R, W=W, out=out, in0=in0, scalar=scalar, in1=in1, op0=op0, op1=op1)

    def cp(eng, out, in_, R, W):
        if eng == "act":
            P.op("act", "activation", R=R, W=W, out=out, in_=in_, func=AF.Copy)
        else:
            P.op(eng, "tensor_copy", R=R, W=W, out=out, in_=in_)

    def memset(eng, ap, val, W):
        P.op(eng, "memset", R=(), W=W, ap=ap, constant=val)

    dma_rr = [0]

    def dma_sp(out, in_, R, W, chan=None, **kw):
        if chan is None:
            chan = "g%d" % (dma_rr[0] % 8)
            dma_rr[0] += 1
        P.dma("sp", out, in_, R, W, chan, **kw)

    def dbg(name, ap):
        if name in DBG:
            dma_sp(DBG[name], ap, R=["*dbg"], W=["dbg_" + name])

    pc = {}
    off = 0
    for nm, n in (("g_pre_mix", 8), ("g_post_mix", 8), ("g_pre_ffn", 8), ("g_post_ffn", 8), ("conv_b", 4),
                  ("conv_ln_g", 4), ("conv_ln_b", 4), ("g_q", 3), ("g_kv", 2), ("ssm_conv_b", 16),
                  ("g_ssm", 8), ("conv_w", 124), ("ssm_conv_w", 64)):
        pc[nm] = (off, n)
        off += L * n
    assert off == NPC

    def pcol(nm, l, c):
        o, n = pc[nm]
        return params[:, o + l * n + c: o + l * n + c + 1]

    dma_sp(ident_f[:], I["c_ident"], R=[], W=["ident_f"])
    cp("dve", ident_b[:], ident_f[:], R=["ident_f"], W=["ident_b"])
    memset("dve", ones_f[:], 1.0, W=["ones_f"])
    memset("dve", ones_b[:], 1.0, W=["ones_b"])
    dma_sp(tri_f[:], I["c_tri"], R=[], W=["tri_f"])
    cp("dve", tri_b[:], tri_f[:], R=["tri_f"], W=["tri_b"])
    dma_sp(tokin[:, 0:128], I["c_lstrict"], R=[], W=["tokin"])
    cp("dve", lst_b[:], tokin[:, 0:128], R=["tokin"], W=["lst_b"])
    memset("dve", wuq_rot[:], 0.0, W=["wuq_rot"])
    memset("dve", wkr_pad[:], 0.0, W=["wkr_pad"])
    memset("dve", wkr_rot[:], 0.0, W=["wkr_rot"])
    memset("dve", Vc[:], 1.0, W=[("Vc", kb_) for kb_ in range(NKB)])
    memset("dve", kr96[:], 0.0, W=["kr96"])

    def load_param_cols(nm, rows_ap, nrows):
        o, n = pc[nm]
        r0 = 0
        while r0 < nrows:
            nr = min(128, nrows - r0)
            dma_sp(tokin[0:nr, 0:128], rows_ap[r0:r0 + nr, :], R=[], W=["tokin"])
            pt, pk = PS()
            tr(pt[:, 0:nr], tokin[0:nr, 0:128], ident_f[0:nr, 0:nr], R=["tokin", "ident_f"], W=[pk])
            cp("dve", params[:, o + r0:o + r0 + nr], pt[:, 0:nr], R=[pk], W=["params"])
            r0 += nr

    for nm in ("g_pre_mix", "g_post_mix", "g_pre_ffn", "g_post_ffn", "conv_b", "conv_ln_g", "conv_ln_b",
               "g_q", "g_kv", "ssm_conv_b", "g_ssm"):
        n = pc[nm][1]
        load_param_cols(nm, I[nm].rearrange("l (c p) -> (l c) p", p=128), L * n)
    load_param_cols("conv_w", I["conv_w"].rearrange("l k (c p) -> (l k c) p", p=128), L * 124)
    load_param_cols("ssm_conv_w", I["ssm_conv_w"].rearrange("l k (c p) -> (l k c) p", p=128), L * 64)
    for i, nm in enumerate(("dt_bias", "a_log", "d_skip")):
        src = I[nm].rearrange("l h -> (l h)").partition_broadcast(128)
        dma_sp(bcp[:, i, :], src, R=[], W=["bcp"])
    act(bcp[:, 1, :], bcp[:, 1, :], AF.Exp, R=["bcp"], W=["bcp"])
    ts("dve", bcp[:, 1, :], bcp[:, 1, :], -1.0, None, ALU.mult, None, R=["bcp"], W=["bcp"])

    sl_rr = [0]

    def w_cols(nm, l, r0, nk, c0, ncol):
        def fn(slot, sk, s):
            src = I[nm][l, r0:r0 + 128 * nk, c0:c0 + ncol].rearrange("(kc p) n -> p kc n", p=128)
            dst = slot[:, 0:nk * ncol].rearrange("p (kc n) -> p kc n", n=ncol)
            P.dma("pool", dst, src, R=[], W=[sk], chan="w%d" % s)

        def d2d(dview, wkey, chan, gate):
            src = I[nm][l, r0:r0 + 128 * nk, c0:c0 + ncol].rearrange("(kc p) n -> p kc n", p=128)
            P.dma("pool", dview[0:128, 0:nk * ncol].rearrange("p (kc n) -> p kc n", n=ncol), src, R=gate, W=[wkey], chan=chan)
        fn.d2d = d2d
        return fn, 128, nk * ncol

    def w_attn(l, c0, ncol):
        def fn(slot, sk, s):
            src = I["w_mix_out"][l, 512:1024, c0:c0 + ncol].rearrange("(h p) n -> p h n", p=64)
            dst = slot[0:64, 0:NH * ncol].rearrange("p (h n) -> p h n", n=ncol)
            P.dma("pool", dst, src, R=[], W=[sk], chan="w%d" % s)

        def d2d(dview, wkey, chan, gate):
            src = I["w_mix_out"][l, 512:1024, c0:c0 + ncol].rearrange("(h p) n -> p h n", p=64)
            P.dma("pool", dview[0:64, 0:NH * ncol].rearrange("p (h n) -> p h n", n=ncol), src, R=gate, W=[wkey], chan=chan)
        fn.d2d = d2d
        return fn, 64, NH * ncol

    def w_diag(l, g):
        def fn(slot, sk, s):
            sv = slot[:].rearrange("p (i q) -> p i q", q=128)
            if g < 4:
                for k in range(CONV_K):
                    ts("dve", sv[:, k, :], ident_b[:], pcol("conv_w", l, k * 4 + g), None, ALU.mult, None,
                       R=["ident_b", "params"], W=[sk])
            else:
                for c4 in range(4):
                    c = (g - 4) * 4 + c4
                    for k in range(4):
                        ts("dve", sv[:, c4 * 4 + k, :], ident_b[:], pcol("ssm_conv_w", l, k * 16 + c), None,
                           ALU.mult, None, R=["ident_b", "params"], W=[sk])
        return fn, 128, (CONV_K if g < 4 else 16) * 128

    def layer_groups(l):
        G = []
        G.append(("glu_v", w_cols("w_in", l, 0, KC, C_GLU, 512)))
        G.append(("glu_g", w_cols("w_in", l, 0, KC, C_GLU + 512, 512)))
        for c in range(4):
            G.append(("cdiag%d" % c, w_diag(l, c)))
        G.append(("q", w_cols("w_in", l, 0, KC, C_Q, Q_LORA)))
        G.append(("kvkr", w_cols("w_in", l, 0, KC, C_KV, KV_LORA + QK_ROPE)))
        G.append(("wuq", w_cols("w_uq", l, 0, 3, 0, NH * QD)))
        G.append(("wukv", w_cols("w_ukv", l, 0, 2, 0, NH * 128)))
        for g in range(4):
            G.append(("xbc%d" % g, w_cols("w_in", l, 0, KC, C_XBC + 512 * g, 512)))
            G.append(("xdiag%d" % g, w_diag(l, 4 + g)))
        G.append(("dt", w_cols("w_in", l, 0, KC, C_DT, SH)))
        for g in range(2):
            G.append(("z%d" % g, w_cols("w_in", l, 0, KC, C_Z + 512 * g, 512)))
        for cb in range(2):
            G.append(("gate0_%d" % cb, w_cols("w_in", l, 0, KC, C_GATE + 512 * cb, 512)))
            G.append(("mconv%d" % cb, w_cols("w_mix_out", l, 0, 4, 512 * cb, 512)))
            G.append(("gate1_%d" % cb, w_cols("w_in", l, 0, KC, C_GATE + 1024 + 512 * cb, 512)))
            G.append(("mattn%d" % cb, w_attn(l, 512 * cb, 512)))
            G.append(("gate2_%d" % cb, w_cols("w_in", l, 0, KC, C_GATE + 2048 + 512 * cb, 512)))
            G.append(("mssm%d" % cb, w_cols("w_mix_out", l, 1024, 8, 512 * cb, 512)))
        for cb in range(2):
            G.append(("wout%d" % cb, w_cols("w_out", l, 0, KC, 512 * cb, 512)))
        for g in range(11):
            G.append(("gu%d" % g, w_cols("w_gate_up", l, 0, KC, 512 * g, 512)))
        for cb in range(2):
            for kg in range(3):
                nk = 8 if kg < 2 else 6
                G.append(("dn%d_%d" % (cb, kg), w_cols("w_down", l, 1024 * kg, nk, 512 * cb, 512)))
        return G

    LG = [layer_groups(l) for l in range(L)]
    pc_rr = [0]

    def precast_layer(l, gate):
        for gi, (name, (fill, np_, ne)) in enumerate(LG[l]):
            if hasattr(fill, "d2d"):
                fill.d2d(dgd[l, gi], ("wsc", l, gi), "pc%d" % (pc_rr[0] % 8), gate)
                pc_rr[0] += 1

    assert len(LG[0]) <= 64
    for l in range(L):
        if l >= 1:
            continue
        precast_layer(l, [])

    class WStream:
        def __init__(self):
            self.plan = []
            self.issued = 0
            self.taken = 0
            self.rr = sl_rr[0]

        def add(self, name, fn):
            self.plan.append((name, fn))

        def _issue(self):
            name, fn = self.plan[self.issued]
            s = (self.rr + self.issued) % NSLOT
            fn(slots[s], "wslot%d" % s, s)
            self.issued += 1

        def next(self, name):
            assert self.plan[self.taken][0] == name, (self.plan[self.taken][0], name)
            while self.issued < min(len(self.plan), self.taken + NSLOT - 1):
                self._issue()
            s = (self.rr + self.taken) % NSLOT
            self.taken += 1
            return slots[s], "wslot%d" % s

    WS = WStream()

    def w_scr(l, gi, np_, ne):
        def fn(slot, sk, s):
            P.dma("sp", slot[0:np_, 0:ne], dgd[l, gi, 0:np_, 0:ne], R=[("wsc", l, gi)], W=[sk], chan="w%d" % s)
        return fn

    def build_store(l, gi, fill, np_, ne):
        def fn(slot, sk, s):
            fill(slot, sk, s)
            dma_sp(dgd[l, gi, 0:np_, 0:ne], slot[0:np_, 0:ne], R=[sk], W=[("wsc", l, gi)], chan="ws%d" % s)
        return fn

    def plan_tile(l, direct=False, first_of_layer=False):
        for gi, (name, (fill, np_, ne)) in enumerate(LG[l]):
            if not hasattr(fill, "d2d"):
                WS.add(name, build_store(l, gi, fill, np_, ne) if first_of_layer else w_scr(l, gi, np_, ne))
            else:
                WS.add(name, fill if direct else w_scr(l, gi, np_, ne))

    tiles = []
    for l in range(L):
        for sq in range(NSEQ):
            for j in range(NT):
                if not os.environ.get('SKIP_PROMPT'):
                    tiles.append((l, "p", sq, j))
        if cfg.sample:
            tiles.append((l, "s", NSEQ, 0))
    for ti_, (l, kind, sq, j) in enumerate(tiles):
        plan_tile(l, direct=(ti_ == 0), first_of_layer=(ti_ == 0 or tiles[ti_ - 1][0] != l))

    def rstd_from(ps_ap, invd, T, fk, fb, R):
        act(fb[:, 0:T], ps_ap, AF.Ln, R=R, W=[fk], scale=invd, bias=EPS)
        act(fb[:, 0:T], fb[:, 0:T], AF.Exp, R=[fk], W=[fk], scale=-0.5)

    def sumsq(chunks, T, dim):
        pt, pk = PS()
        n = len(chunks)
        for i, (ap, k) in enumerate(chunks):
            sb_, sk_ = bpool.get()
            act(sb_[:, 0:T], ap, AF.Square, R=[k], W=[sk_])
            mm(pt[:, 0:T], ones_b[:], sb_[:, 0:T], i == 0, i == n - 1, R=["ones_b", sk_], W=[pk])
            bpool.put((sb_, sk_))
        fb, fk = fpool.get()
        rstd_from(pt[:, 0:T], 1.0 / dim, T, fk, fb, [pk])
        return fb, fk

    seqst = {"nkeys": 0}
    xnext = [0]

    def load_x_tile(tile, xb, xk):
        (l, kind, sq, j) = tile
        T = TT if kind == "p" else DEC_T
        if l == 0:
            src = I["x_prompt"][sq] if kind == "p" else I["x_sample"][0]
            p0 = j * TT if kind == "p" else 0
            b0 = 0
            while b0 < T:
                nb = min(128, T - b0)
                dma_sp(tokin[0:nb, :], src[p0 + b0:p0 + b0 + nb, :], R=[], W=["tokin"])
                for half in range(2):
                    pt, pk = PS()
                    for c4 in range(4):
                        c = half * 4 + c4
                        tr(pt[:, c4 * 128:c4 * 128 + nb], tokin[0:nb, c * 128:(c + 1) * 128], ident_f[0:nb, 0:nb],
                           R=["tokin", "ident_f"], W=[pk])
                    cp("act", xb[:, half * 4:half * 4 + 4, b0:b0 + nb],
                       pt[:, :].rearrange("p (c t) -> p c t", t=128)[:, :, 0:nb], R=[pk], W=[xk])
                b0 += nb
        else:
            p0 = j * TT if kind == "p" else 0
            dma_sp(xb[:, :, 0:T], xres[sq, :, :, p0:p0 + T], R=[("xres", sq, j)], W=[xk])

    def do_tile(ti):
        tile = tiles[ti]
        (l, kind, sq, j) = tile
        T = TT if kind == "p" else DEC_T
        pos0 = j * TT if kind == "p" else 0
        tcol0 = pos0 if kind == "p" else SEQ
        key0 = pos0 if kind == "p" else PAST
        first = (j == 0)
        last = (kind == "s") or (j == NT - 1)
        NB = (T + 127) // 128
        xb = xbuf[ti % 2]
        xk = "xT%d" % (ti % 2)
        load_x_tile(tile, xb, xk)
        lfirst = (ti == 0 or tiles[ti - 1][0] != l)
        dma_sp(rope_t[:, 0, 0:T], I["c_rcos"][:, tcol0:tcol0 + T], R=[], W=["rope_t"] + ([("gate", l)] if lfirst else []))
        if lfirst and l + 1 < L:
            precast_layer(l + 1, [("gate", l)])
        dma_sp(rope_t[:, 1, 0:T], I["c_rsin"][:, tcol0:tcol0 + T], R=[], W=["rope_t"])
        COS = rope_t[:, 0, 0:T]
        SIN = rope_t[:, 1, 0:T]

        if first:
            if kind == "p":
                memset("dve", a_buf[:, :, 0:30], 0.0, W=["a_buf"])
                memset("dve", xhist[:], 0.0, W=["xhist"])
                memset("dve", state_f[:], 0.0, W=["state_f"])
                memset("dve", state_b[:], 0.0, W=["state_b"])
            else:
                dma_sp(st30[0:30, :], I["state_conv"][l, 0], R=[], W=["st30"])
                pt, pk = PS()
                for c in range(4):
                    tr(pt[:, c * 32:c * 32 + 30], st30[0:30, c * 128:(c + 1) * 128], ident_f[0:30, 0:30],
                       R=["st30", "ident_f"], W=[pk])
                cp("act", a_buf[:, :, 0:30], pt[:, 0:128].rearrange("p (c t) -> p c t", t=32)[:, :, 0:30],
                   R=[pk], W=["a_buf"])
                if int(os.environ.get('INITSTOP', '99')) <= 1:
                    raise _Stop()
                if not os.environ.get("SKIP_D2D"):
                    dma_sp(O["conv_s"][l, 0, 0:14, :], I["state_conv"][l, 0, 16:30, :], R=[], W=[("o_conv_s", l)])
                if int(os.environ.get('INITSTOP', '99')) <= 2:
                    raise _Stop()
                dma_sp(tokin[0:3, 0:1024], I["state_ssm_conv"][l, 0, :, 0:1024], R=[], W=["tokin"])
                dma_sp(tokout[0:3, 0:1024], I["state_ssm_conv"][l, 0, :, 1024:2048], R=[], W=["tokout"])
                pt, pk = PS()
                for c in range(16):
                    srcb = tokin if c < 8 else tokout
                    tr(pt[:, c * 4:c * 4 + 3], srcb[0:3, (c % 8) * 128:(c % 8 + 1) * 128], ident_f[0:3, 0:3],
                       R=["tokin", "tokout", "ident_f"], W=[pk])
                cp("act", xhist[:], pt[:, 0:64].rearrange("p (c t) -> p c t", t=4)[:, :, 0:3], R=[pk], W=["xhist"])
                if int(os.environ.get('INITSTOP', '99')) <= 3:
                    raise _Stop()
                src = I["state_ssm"][l, 0].rearrange("(c h) p n -> (h p) c n", h=2)
                dma_sp(tokin[:, :].rearrange("q (c n) -> q c n", n=128), src, R=[], W=["tokin"])
                for half in range(2):
                    pt, pk = PS()
                    for c4 in range(4):
                        c = half * 4 + c4
                        tr(pt[:, c4 * 128:(c4 + 1) * 128], tokin[:, c * 128:(c + 1) * 128], ident_f[:],
                           R=["tokin", "ident_f"], W=[pk])
                    cp("act", state_f[:, half * 512:(half + 1) * 512], pt[:, :], R=[pk], W=["state_f"])
                cp("dve", state_b[:], state_f[:], R=["state_f"], W=["state_b"])
                if int(os.environ.get('INITSTOP', '99')) <= 4:
                    raise _Stop()
                latc = [bpool.get() for _ in range(2 * (PAST // TT))]
                for kb in range(PAST // 128):
                    dma_sp(tokin[:, 0:256], I["cache_mla_latent"][l, 0, kb * 128:(kb + 1) * 128, :], R=[], W=["tokin"])
                    if not os.environ.get("SKIP_KRC"):
                        dma_sp(kr96[:, 64:96], I["cache_mla_rope"][l, 0, kb * 128:(kb + 1) * 128, :], R=[], W=["kr96"])
                    pt, pk = PS()
                    for c in range(2):
                        tr(pt[:, c * 128:(c + 1) * 128], tokin[:, c * 128:(c + 1) * 128], ident_f[:],
                           R=["tokin", "ident_f"], W=[pk])
                    if not os.environ.get("SKIP_KRT"):
                        tr(pt[0:QD, 256:384], kr96[:, :], ident_f[:], R=["kr96", "ident_f"], W=[pk])
                    u = (kb * 128) // TT
                    o = (kb * 128) % TT
                    for c in range(2):
                        lb, lk = latc[c * (PAST // TT) + u]
                        cp("act", lb[:, o:o + 128], pt[:, c * 128:(c + 1) * 128], R=[pk], W=[lk])
                    cp("act", krf[64:96, 0:128], pt[64:96, 256:384], R=[pk], W=["krf"])
                    for h in range(NH):
                        cp("dve" if h % 2 else "act", Kc[64:96, h, kb * 128:(kb + 1) * 128], krf[64:96, 0:128],
                           R=["krf"], W=[("Kc", h)])
                seqst["latc"] = latc
            seqst["nkeys"] = 0 if kind == "p" else PAST

        if kind == 's' and cfg.stop <= 0:
            raise _Stop()
        fb, fk = sumsq([(xb[:, c, 0:T], xk) for c in range(KC)], T, D)
        hT = [bpool.get() for _ in range(KC)]
        for c in range(KC):
            stt("dve", hT[c][0][:, 0:T], xb[:, c, 0:T], pcol("g_pre_mix", l, c), fb[:, 0:T], ALU.mult, ALU.mult,
                R=[xk, fk, "params"], W=[hT[c][1]])
        fpool.put((fb, fk))
        hk = [h[1] for h in hT]

        def win_chunk(ws, wk, nk, ncol, j0, M, pt_ap, pk):
            wv = ws[:, 0:nk * ncol].rearrange("p (kc n) -> p kc n", n=ncol)
            for kc in range(nk):
                mm(pt_ap, wv[:, kc, j0:j0 + M], hT[kc][0][:, 0:T], kc == 0, kc == nk - 1, R=[wk, hk[kc]], W=[pk])

        def tokmajor_tail(ws, wk, ncol, ntok, pt, pk, c0=0, n=None):
            n = ncol if n is None else n
            wv = ws[:, 0:KC * ncol].rearrange("p (kc n) -> p kc n", n=ncol)
            for kc in range(KC):
                mm(pt[0:ntok, 0:n], hT[kc][0][:, T - ntok:T], wv[:, kc, c0:c0 + n], kc == 0, kc == KC - 1,
                   R=[wk, hk[kc]], W=[pk])

        if kind == 's' and cfg.stop <= 1:
            raise _Stop()
        ntail = min(30, T)
        ws, wk = WS.next("glu_v")
        vps = []
        for c in range(4):
            pt, pk = PS()
            win_chunk(ws, wk, KC, 512, c * 128, 128, pt[:, 0:T], pk)
            vps.append((pt, pk))
        if last:
            ptv, pkv = PS()
            tokmajor_tail(ws, wk, 512, ntail, ptv, pkv)
            cp("act", st30[0:ntail, :], ptv[0:ntail, :], R=[pkv], W=["st30"])
        vfs = []
        for c in range(4):
            vb, vk = fpool.get()
            cp("act" if c % 2 else "dve", vb[:, 0:T], vps[c][0][:, 0:T], R=[vps[c][1]], W=[vk])
            vfs.append((vb, vk))
        ws, wk = WS.next("glu_g")
        for c in range(4):
            pt, pk = PS()
            win_chunk(ws, wk, KC, 512, c * 128, 128, pt[:, 0:T], pk)
            sb_, sk_ = fpool.get()
            act(sb_[:, 0:T], pt[:, 0:T], AF.Sigmoid, R=[pk], W=[sk_])
            tt("dve", a_buf[:, c, 30:30 + T], vfs[c][0][:, 0:T], sb_[:, 0:T], ALU.mult, R=[vfs[c][1], sk_], W=["a_buf"])
            fpool.put((sb_, sk_))
            fpool.put(vfs[c])
        if last:
            ptg, pkg = PS()
            tokmajor_tail(ws, wk, 512, ntail, ptg, pkg)
            act(tokout[0:ntail, 0:512], ptg[0:ntail, :], AF.Sigmoid, R=[pkg], W=["tokout"])
            tt("dve", tokout[0:ntail, 0:512], tokout[0:ntail, 0:512], st30[0:ntail, :], ALU.mult,
               R=["tokout", "st30"], W=["tokout"])
            if kind == "p":
                dma_sp(O["conv_p"][l, sq], tokout[0:30, 0:512], R=["tokout"], W=[("o_conv_p", l, sq)])
            else:
                dma_sp(O["conv_s"][l, 0, 14:30, :], tokout[0:16, 0:512], R=["tokout"], W=[("o_conv_s2", l)])
        cos_ = []
        for c in range(4):
            ws, wk = WS.next("cdiag%d" % c)
            dv = ws[:].rearrange("p (i q) -> p i q", q=128)
            pt, pk = PS()
            for k in range(CONV_K):
                mm(pt[:, 0:T], dv[:, k, :], a_buf[:, c, k:k + T], k == 0, k == CONV_K - 1, R=[wk, "a_buf"], W=[pk])
            cb_, ck_ = fpool.get()
            act(cb_[:, 0:T], pt[:, 0:T], AF.Identity, R=[pk, "params"], W=[ck_], bias=pcol("conv_b", l, c))
            cos_.append((cb_, ck_))
        if not last:
            for c in range(4):
                cp("pool", a_buf[:, c, 0:30], a_buf[:, c, T:T + 30], R=["a_buf"], W=["a_buf"])
        pt1, pk1 = PS()
        pt2, pk2 = PS()
        for c in range(4):
            b1, k1 = bpool.get()
            b2, k2 = bpool.get()
            cp("dve", b1[:, 0:T], cos_[c][0][:, 0:T], R=[cos_[c][1]], W=[k1])
            act(b2[:, 0:T], cos_[c][0][:, 0:T], AF.Square, R=[cos_[c][1]], W=[k2])
            mm(pt1[:, 0:T], ones_b[:], b1[:, 0:T], c == 0, c == 3, R=["ones_b", k1], W=[pk1])
            mm(pt2[:, 0:T], ones_b[:], b2[:, 0:T], c == 0, c == 3, R=["ones_b", k2], W=[pk2])
            bpool.put((b1, k1))
            bpool.put((b2, k2))
        mb, mk = fpool.get()
        rb, rk = fpool.get()
        ts("dve", mb[:, 0:T], pt1[:, 0:T], 1.0 / D_CONV, None, ALU.mult, None, R=[pk1], W=[mk])
        tt("dve", rb[:, 0:T], mb[:, 0:T], mb[:, 0:T], ALU.mult, R=[mk], W=[rk])
        stt("dve", rb[:, 0:T], pt2[:, 0:T], 1.0 / D_CONV, rb[:, 0:T], ALU.mult, ALU.subtract, R=[pk2, rk], W=[rk])
        act(rb[:, 0:T], rb[:, 0:T], AF.Ln, R=[rk], W=[rk], bias=EPS)
        act(rb[:, 0:T], rb[:, 0:T], AF.Exp, R=[rk], W=[rk], scale=-0.5)
        cT = []
        for c in range(4):
            cb_, ck_ = cos_[c]
            tt("dve", cb_[:, 0:T], cb_[:, 0:T], mb[:, 0:T], ALU.subtract, R=[ck_, mk], W=[ck_])
            tt("dve", cb_[:, 0:T], cb_[:, 0:T], rb[:, 0:T], ALU.mult, R=[ck_, rk], W=[ck_])
            ob, ok = bpool.get()
            act(ob[:, 0:T], cb_[:, 0:T], AF.Silu, R=[ck_, "params"], W=[ok],
                scale=pcol("conv_ln_g", l, c), bias=pcol("conv_ln_b", l, c))
            cT.append((ob, ok))
            fpool.put((cb_, ck_))
        fpool.put((mb, mk))
        fpool.put((rb, rk))

        if kind == 's' and cfg.stop <= 2:
            raise _Stop()
        ws, wk = WS.next("q")
        uq = []
        for c in range(3):
            pt, pk = PS()
            win_chunk(ws, wk, KC, Q_LORA, c * 128, 128, pt[:, 0:T], pk)
            ub, uk = fpool.get()
            cp("act", ub[:, 0:T], pt[:, 0:T], R=[pk], W=[uk])
            uq.append((ub, uk))
        fb, fk = sumsq([(u[0][:, 0:T], u[1]) for u in uq], T, Q_LORA)
        qn = []
        for c in range(3):
            qb, qk = bpool.get()
            stt("dve", qb[:, 0:T], uq[c][0][:, 0:T], pcol("g_q", l, c), fb[:, 0:T], ALU.mult, ALU.mult,
                R=[uq[c][1], fk, "params"], W=[qk])
            qn.append((qb, qk))
            fpool.put(uq[c])
        fpool.put((fb, fk))

        ws, wk = WS.next("kvkr")
        NCKV = KV_LORA + QK_ROPE
        wv = ws[:, 0:KC * NCKV].rearrange("p (kc n) -> p kc n", n=NCKV)
        cp("act", wkr_pad[:, :, 64:96], wv[:, :, 256:288], R=[wk], W=["wkr_pad"])
        cp("act", wkr_rot[:, :, 80:96], wv[:, :, 256:272], R=[wk], W=["wkr_rot"])
        ts("dve", wkr_rot[:, :, 64:80], wv[:, :, 272:288], -1.0, None, ALU.mult, None, R=[wk], W=["wkr_rot"])
        ukv = []
        for c in range(2):
            pt, pk = PS()
            win_chunk(ws, wk, KC, NCKV, c * 128, 128, pt[:, 0:T], pk)
            ub, uk = fpool.get()
            cp("act", ub[:, 0:T], pt[:, 0:T], R=[pk], W=[uk])
            ukv.append((ub, uk))
        ptA, pkA = PS()
        ptB, pkB = PS()
        for kc in range(KC):
            mm(ptA[0:QD, 0:T], wkr_pad[:, kc, :], hT[kc][0][:, 0:T], kc == 0, kc == KC - 1, R=["wkr_pad", hk[kc]], W=[pkA])
        for kc in range(KC):
            mm(ptB[0:QD, 0:T], wkr_rot[:, kc, :], hT[kc][0][:, 0:T], kc == 0, kc == KC - 1, R=["wkr_rot", hk[kc]], W=[pkB])
        tt("dve", qt1[64:96, 0:T], ptB[64:96, 0:T], SIN[64:96, :], ALU.mult, R=[pkB, "rope_t"], W=["qt1"])
        tt("dve", qt2[64:96, 0:T], ptA[64:96, 0:T], COS[64:96, :], ALU.mult, R=[pkA, "rope_t"], W=["qt2"])
        tt("dve", krf[64:96, 0:T], qt1[64:96, 0:T], qt2[64:96, 0:T], ALU.add, R=["qt1", "qt2"], W=["krf"])
        nk0 = seqst["nkeys"]
        for h in range(NH):
            cp("act" if h % 2 else "dve", Kc[64:96, h, nk0:nk0 + T], krf[64:96, 0:T], R=["krf"], W=[("Kc", h)])
        for b in range(NB):
            nb = min(128, T - b * 128)
            pt, pk = PS()
            tr(pt[0:nb, 0:32], krf[64:96, b * 128:b * 128 + nb], ident_f[64:96, 64:96], R=["krf", "ident_f"], W=[pk])
            cp("act", krT[0:nb, 0, :], pt[0:nb, 0:32], R=[pk], W=["krT0"])
            dst = (O["kr_p"][l, sq, pos0 + b * 128:pos0 + b * 128 + nb, :] if kind == "p"
                   else O["kr_s"][l, 0, 0:nb, :])
            dma_sp(dst, krT[0:nb, 0, :], R=["krT0"], W=[("o_kr", l, sq, j, b)])
        fb, fk = sumsq([(u[0][:, 0:T], u[1]) for u in ukv], T, KV_LORA)
        latb = []
        for c in range(2):
            ub, uk = ukv[c]
            stt("dve", ub[:, 0:T], ub[:, 0:T], pcol("g_kv", l, c), fb[:, 0:T], ALU.mult, ALU.mult,
                R=[uk, fk, "params"], W=[uk])
            lb, lk = bpool.get()
            cp("pool", lb[:, 0:T], ub[:, 0:T], R=[uk], W=[lk])
            latb.append((lb, lk))
        fpool.put((fb, fk))
        for b in range(NB):
            nb = min(128, T - b * 128)
            pt, pk = PS()
            for c in range(2):
                tr(pt[0:nb, c * 128:(c + 1) * 128], ukv[c][0][:, b * 128:b * 128 + nb], ident_f[:],
                   R=[ukv[c][1], "ident_f"], W=[pk])
            cp("act", latT[0:nb, 0, :], pt[0:nb, 0:256], R=[pk], W=["latT0"])
            dst = (O["lat_p"][l, sq, pos0 + b * 128:pos0 + b * 128 + nb, :] if kind == "p"
                   else O["lat_s"][l, 0, 0:nb, :])
            dma_sp(dst, latT[0:nb, 0, :], R=["latT0"], W=[("o_lat", l, sq, j, b)])
        for c in range(2):
            fpool.put(ukv[c])

        if kind == 's' and cfg.stop <= 3:
            raise _Stop()
        ws_q, wk_q = WS.next("wuq")
        wq = ws_q[:, 0:3 * NH * QD].rearrange("p (kc n) -> p kc n", n=NH * QD)
        wq4 = ws_q[:, 0:3 * NH * QD].rearrange("p (kc h d) -> p kc h d", h=NH, d=QD)
        wr4 = wuq_rot[:].rearrange("p kc (h d) -> p kc h d", d=QD)
        for kc in range(3):
            cp("act", wr4[:, kc, :, 80:96], wq4[:, kc, :, 64:80], R=[wk_q], W=["wuq_rot"])
            ts("dve", wr4[:, kc, :, 64:80], wq4[:, kc, :, 80:96], -1.0, None, ALU.mult, None, R=[wk_q], W=["wuq_rot"])
        for h in range(NH):
            ptA, pkA = PS()
            ptB, pkB = PS()
            for kc in range(3):
                mm(ptA[0:QD, 0:T], wq[:, kc, h * QD:(h + 1) * QD], qn[kc][0][:, 0:T], kc == 0, kc == 2,
                   R=[wk_q, qn[kc][1]], W=[pkA])
            for kc in range(3):
                mm(ptB[0:QD, 0:T], wuq_rot[:, kc, h * QD:(h + 1) * QD], qn[kc][0][:, 0:T], kc == 0, kc == 2,
                   R=["wuq_rot", qn[kc][1]], W=[pkB])
            tt("dve", qt1[:, 0:T], ptB[0:QD, 0:T], SIN, ALU.mult, R=[pkB, "rope_t"], W=["qt1"])
            tt("dve", qt2[:, 0:T], ptA[0:QD, 0:T], COS, ALU.mult, R=[pkA, "rope_t"], W=["qt2"])
            tt("dve", Qp[:, h, 0:T], qt1[:, 0:T], qt2[:, 0:T], ALU.add, R=["qt1", "qt2"], W=[("Qp", h)])
        for c in range(3):
            bpool.put(qn[c])

        if kind == 's' and cfg.stop <= 4:
            raise _Stop()
        ws, wk = WS.next("wukv")
        wkv = ws[:, 0:2 * NH * 128].rearrange("p (kc n) -> p kc n", n=NH * 128)

        def kv_gen(lat_chunks, Tn, kbase):
            for h in range(0, NH, 2):
                pt, pk = PS()
                for a_ in range(2):
                    for kc in range(2):
                        mm(pt[0:64, a_ * TT:a_ * TT + Tn], wkv[:, kc, (h + a_) * 128:(h + a_) * 128 + 64], lat_chunks[kc][0],
                           kc == 0, kc == 1, R=[wk, lat_chunks[kc][1]], W=[pk])
                cp("act" if (h // 2) % 2 else "dve", Kc[0:64, h:h + 2, kbase:kbase + Tn],
                   pt[0:64, 0:2 * TT].rearrange("p (a t) -> p a t", t=TT)[:, :, 0:Tn], R=[pk], W=[("Kc", h), ("Kc", h + 1)])
            nblk = (Tn + 127) // 128
            for b in range(nblk):
                nb = min(128, Tn - b * 128)
                kb = (kbase + b * 128) // 128
                for half in range(2):
                    pt, pk = PS()
                    for kc in range(2):
                        mm(pt[0:nb, :], lat_chunks[kc][0][:, b * 128:b * 128 + nb], wkv[:, kc, half * 512:(half + 1) * 512],
                           kc == 0, kc == 1, R=[wk, lat_chunks[kc][1]], W=[pk])
                    cp("act" if half else "dve", Vc[0:nb, kb, half * 4:half * 4 + 4, 0:64],
                       pt[0:nb, :].rearrange("p (h d) -> p h d", d=128)[:, :, 64:128], R=[pk], W=[("Vc", kb)])

        if kind == "s":
            latc = seqst["latc"]
            nu = PAST // TT
            for u in range(nu):
                kv_gen([(latc[c * nu + u][0][:, 0:TT], latc[c * nu + u][1]) for c in range(2)], TT, u * TT)
            for b_ in latc:
                bpool.put(b_)
        kv_gen([(latb[c][0][:, 0:T], latb[c][1]) for c in range(2)], T, nk0)
        for c in range(2):
            bpool.put(latb[c])
        nkeys = nk0 + T
        seqst["nkeys"] = nkeys

        if kind == 's' and cfg.stop <= 5:
            raise _Stop()
        nkb = (nkeys + 127) // 128
        pb_rr = 0
        for hp in range(NH // 2):
            hs = (2 * hp, 2 * hp + 1)
            po0, pok0 = PS()
            po1, pok1 = PS(excl=(pok0,))
            pos_ = ((po0, pok0), (po1, pok1))
            for kb in range(nkb):
                ks = kb * 128
                nkk = min(128, nkeys - ks)
                if kind == "p" and ks >= key0:
                    q0 = ks - key0
                else:
                    q0 = 0
                nq = T - q0
                pt, pk = PS(excl=(pok0, pok1))
                for a_, h in enumerate(hs):
                    mm(pt[0:nkk, a_ * TT:a_ * TT + nq], Kc[:, h, ks:ks + nkk], Qp[:, h, q0:T], True, True,
                       R=[("Kc", h), ("Qp", h)], W=[pk])
                pb = pbufs[pb_rr % NPB]
                pbk = "pbuf%d" % (pb_rr % NPB)
                pb_rr += 1
                act(pb[0:nkk, :, 0:nq], pt[0:nkk, 0:2 * TT].rearrange("p (a t) -> p a t", t=TT)[:, :, 0:nq], AF.Exp,
                    R=[pk], W=[pbk], scale=ATTN_SCALE)
                if kind == "p" and ks >= key0:
                    memset("dve", pb[64:128, :, 0:64], 0.0, W=[pbk])
                for a_, h in enumerate(hs):
                    mm(pos_[a_][0][0:65, q0:T], Vc[0:nkk, kb, h, 0:65], pb[0:nkk, a_, 0:nq], kb == 0, kb == nkb - 1,
                       R=[("Vc", kb), pbk], W=[pos_[a_][1]])
            for a_, h in enumerate(hs):
                po, pok = pos_[a_]
                P.op("dve", "reciprocal", R=[pok], W=["rden"], out=rden[64:65, 0:T], in_=po[64:65, 0:T])
                pb2, pbk2 = PS(excl=(pok0, pok1))
                mm(pb2[0:64, 0:T], ones_f[64:65, 0:64], rden[64:65, 0:T], True, True, R=["ones_f", "rden"], W=[pbk2])
                cp("act", bcs[:, 0:T], pb2[0:64, 0:T], R=[pbk2], W=["bcs"])
                tt("dve", On[:, h, 0:T], po[0:64, 0:T], bcs[:, 0:T], ALU.mult, R=[pok, "bcs"], W=[("On", h)])

        if kind == 's' and cfg.stop <= 6:
            raise _Stop()
        xact = [None] * 16
        for g in range(4):
            ws, wk = WS.next("xbc%d" % g)
            wsd, wkd = WS.next("xdiag%d" % g)
            dvx = wsd[:].rearrange("p (i q) -> p i q", q=128)
            rb_ = rawb[g % 2]
            rk_ = "rawb%d" % (g % 2)
            cp("dve", rb_[:, :, 0:3], xhist[:, g * 4:g * 4 + 4, :], R=["xhist"], W=[rk_])
            for c4 in (0, 2):
                pt, pk = PS()
                win_chunk(ws, wk, KC, 512, c4 * 128, 128, pt[:, 0:T], pk)
                win_chunk(ws, wk, KC, 512, (c4 + 1) * 128, 128, pt[:, TT:TT + T], pk)
                cp("act" if c4 else "dve", rb_[:, c4:c4 + 2, 3:3 + T],
                   pt[:, 0:2 * TT].rearrange("p (a t) -> p a t", t=TT)[:, :, 0:T], R=[pk], W=[rk_])
            if last:
                pt, pk = PS()
                tokmajor_tail(ws, wk, 512, 3, pt, pk)
                cp("act", st30[0:3, :], pt[0:3, :], R=[pk], W=["st30"])
                dsts_ = O["sconv_p"][l, sq] if kind == "p" else O["sconv_s"][l, 0]
                dma_sp(dsts_[:, g * 512:(g + 1) * 512], st30[0:3, :], R=["st30"], W=[("o_sconv", l, sq, g)])
            else:
                cp("dve", xhist[:, g * 4:g * 4 + 4, :], rb_[:, :, T:T + 3], R=[rk_], W=["xhist"])
            for c4 in range(4):
                c = g * 4 + c4
                pt, pk = PS()
                for k in range(4):
                    mm(pt[:, 0:T], dvx[:, (c4 * 4 + k), :], rb_[:, c4, k:k + T], k == 0, k == 3, R=[wkd, rk_], W=[pk])
                ob, ok = bpool.get()
                act(ob[:, 0:T], pt[:, 0:T], AF.Silu, R=[pk, "params"], W=[ok], bias=pcol("ssm_conv_b", l, c))
                xact[c] = (ob, ok)
        if kind == 's' and cfg.stop <= 7:
            raise _Stop()
        ws, wk = WS.next("dt")
        wdt = ws[:, 0:KC * SH].rearrange("p (kc n) -> p kc n", n=SH)
        dts = []
        for b in range(NB):
            nb = min(128, T - b * 128)
            pt, pk = PS()
            for kc in range(KC):
                mm(pt[0:nb, 0:SH], hT[kc][0][:, b * 128:b * 128 + nb], wdt[:, kc, :], kc == 0, kc == KC - 1,
                   R=[wk, hk[kc]], W=[pk])
            db, dk = fpool.get()
            tt("dve", db[0:nb, 0:SH], pt[0:nb, 0:SH], bcp[0:nb, 0, l * SH:(l + 1) * SH], ALU.add, R=[pk, "bcp"], W=[dk])
            act(db[0:nb, 0:SH], db[0:nb, 0:SH], AF.Exp, R=[dk], W=[dk])
            act(db[0:nb, 0:SH], db[0:nb, 0:SH], AF.Ln, R=[dk], W=[dk], bias=1.0)
            dts.append((db, dk))

        if kind == 's' and cfg.stop <= 8:
            raise _Stop()
        for b in range(NB):
            nb = min(128, T - b * 128)
            t0 = b * 128
            db, dk = dts[b]
            dt_ap = db[0:nb, 0:SH]
            da = sm[0:nb, 1, :]
            tt("dve", da, dt_ap, bcp[0:nb, 1, l * SH:(l + 1) * SH], ALU.mult, R=[dk, "bcp"], W=["sm_da"])
            pcs, pkcs = PS()
            mm(pcs[0:nb, 0:SH], tri_f[0:nb, 0:nb], da, True, True, R=["tri_f", "sm_da"], W=[pkcs])
            mm(pcs[:, SH:2 * SH], ones_f[0:nb, :], da, True, True, R=["ones_f", "sm_da"], W=[pkcs])
            cs = sm[0:nb, 2, :]
            cp("dve", cs, pcs[0:nb, 0:SH], R=[pkcs], W=["sm_cs"])
            ecs = sm[0:nb, 3, :]
            act(ecs, pcs[0:nb, 0:SH], AF.Exp, R=[pkcs], W=["sm_ecs"])
            dte = sm[0:nb, 4, :]
            tt("dve", dte, pcs[0:nb, SH:2 * SH], cs, ALU.subtract, R=[pkcs, "sm_cs"], W=["sm_dte"])
            act(dte, dte, AF.Exp, R=["sm_dte"], W=["sm_dte"])
            etot = sm[:, 5, :]
            act(etot, pcs[:, SH:2 * SH], AF.Exp, R=[pkcs], W=["sm_etot"])
            pxs, pkxs = PSB()
            for c in range(8):
                tr(pxs[0:nb, c * 128:(c + 1) * 128], xact[c][0][:, t0:t0 + nb], ident_b[:], R=[xact[c][1], "ident_b"], W=[pkxs])
            xs3 = pxs[0:nb, :].rearrange("p (h d) -> p h d", d=HP)
            tt("dve", xdt_t[0:nb], xs3, bc_ap(db, nb, [[1, SH], [0, HP]]), ALU.mult, R=[pkxs, dk], W=["xdt"])
            tt("dve", xD_t[0:nb, :].rearrange("p (h d) -> p h d", d=HP), xs3,
               bc_ap(bcp, nb, [[1, SH], [0, HP]], offset=2 * L * SH + l * SH), ALU.mult, R=[pkxs, "bcp"], W=["xD"])
            tt("dve", xdte_t[0:nb], xdt_t[0:nb], bc_ap(sm, nb, [[1, SH], [0, HP]], offset=4 * SH), ALU.mult,
               R=["xdt", "sm_dte"], W=["xdte"])
            pbt, pkbt = PSB()
            for g in range(SG):
                tr(pbt[0:nb, g * 128:(g + 1) * 128], xact[8 + g][0][:, t0:t0 + nb], ident_b[:],
                   R=[xact[8 + g][1], "ident_b"], W=[pkbt])
            cp("act", Btm_t[0:nb, :], pbt[0:nb, 0:512], R=[pkbt], W=["Btm"])
            tt("dve", R_t[0:nb, :, 0:nb], bc_ap(tri_b, nb, [[0, SH], [1, nb]]),
               bc_ap(sm, nb, [[1, SH], [0, nb]], offset=1 * SH), ALU.mult, R=["tri_b", "sm_da"], W=["R_t"])
            hpb = min(SH, max(1, 512 // nb))
            for h0 in range(0, SH, hpb):
                pt, pk = PS()
                mm(pt[0:nb, 0:hpb * nb], lst_b[0:nb, 0:nb], R_t[0:nb, h0:h0 + hpb, 0:nb], True, True,
                   R=["lst_b", "R_t"], W=[pk])
                act(eD_t[0:nb, h0:h0 + hpb, 0:nb], pt[0:nb, 0:hpb * nb].rearrange("p (h l) -> p h l", l=nb), AF.Exp,
                    R=[pk], W=["eD_t"])
            pg, pkg_ = PS()
            for g in range(SG):
                mm(pg[0:nb, g * nb:(g + 1) * nb], xact[8 + g][0][:, t0:t0 + nb], xact[12 + g][0][:, t0:t0 + nb], True, True,
                   R=[xact[8 + g][1], xact[12 + g][1]], W=[pkg_])
            tt("dve", Gm_t[0:nb, :, 0:nb], pg[0:nb, 0:SG * nb].rearrange("p (g l) -> p g l", l=nb),
               bc_ap(tri_b, nb, [[0, SG], [1, nb]]), ALU.mult, R=[pkg_, "tri_b"], W=["Gm_t"])
            for g in range(SG):
                tt("dve", eD_t[0:nb, g * 4:(g + 1) * 4, 0:nb], eD_t[0:nb, g * 4:(g + 1) * 4, 0:nb],
                   bc_ap(Gm_t, nb, [[0, 4], [1, nb]], offset=g * 128), ALU.mult, R=["eD_t", "Gm_t"], W=["eD_t"])
            pyd = [PS(), PS()]
            for h in range(SH):
                pt, pk = pyd[h // 8]
                mm(pt[0:nb, (h % 8) * HP:(h % 8 + 1) * HP], eD_t[0:nb, h, 0:nb], xdt_t[0:nb, h, :], True, True,
                   R=["eD_t", "xdt"], W=[pk])
            pyo = [PS(), PS()]
            for g in range(SG):
                pt, pk = pyo[g // 2]
                mm(pt[0:nb, (g % 2) * 256:(g % 2 + 1) * 256], xact[12 + g][0][:, t0:t0 + nb], state_b[:, g * 256:(g + 1) * 256],
                   True, True, R=[xact[12 + g][1], "state_b"], W=[pk])
            for half in range(2):
                ysl = yt_t[0:nb, half * 512:(half + 1) * 512]
                tt("dve", ysl.rearrange("p (h d) -> p h d", d=HP),
                   pyo[half][0][0:nb, :].rearrange("p (h d) -> p h d", d=HP),
                   bc_ap(sm, nb, [[1, 8], [0, HP]], offset=3 * SH + half * 8), ALU.mult,
                   R=[pyo[half][1], "sm_ecs"], W=["yt_t"])
                tt("dve", ysl, ysl, pyd[half][0][0:nb, :], ALU.add, R=["yt_t", pyd[half][1]], W=["yt_t"])
                tt("dve", ybf_t[0:nb, half * 512:(half + 1) * 512], ysl, xD_t[0:nb, half * 512:(half + 1) * 512], ALU.add,
                   R=["yt_t", "xD"], W=["ybf_t"])
            pyt, pkyt = PSB()
            for c in range(8):
                tr(pyt[:, c * 128:c * 128 + nb], ybf_t[0:nb, c * 128:(c + 1) * 128], ident_b[0:nb, 0:nb],
                   R=["ybf_t", "ident_b"], W=[pkyt])
            cp("act", yT[:, :, t0:t0 + nb], pyt[:, :].rearrange("p (c t) -> p c t", t=128)[:, :, 0:nb], R=[pkyt], W=["yT"])
            pst = [PS(), PS()]
            for g in range(SG):
                pt, pk = pst[g // 2]
                mm(pt[:, (g % 2) * 256:(g % 2 + 1) * 256], Btm_t[0:nb, g * 128:(g + 1) * 128],
                   xdte_t[0:nb, g * 4:(g + 1) * 4, :], True, True, R=["Btm", "xdte"], W=[pk])
            tt("dve", state_f[:, :].rearrange("p (h d) -> p h d", d=HP), state_f[:, :].rearrange("p (h d) -> p h d", d=HP),
               bc_ap(sm, 128, [[1, SH], [0, HP]], offset=5 * SH), ALU.mult, R=["state_f", "sm_etot"], W=["state_f"])
            for half in range(2):
                tt("dve", state_f[:, half * 512:(half + 1) * 512], state_f[:, half * 512:(half + 1) * 512],
                   pst[half][0][:, :], ALU.add, R=["state_f", pst[half][1]], W=["state_f"])
            cp("pool", state_b[:], state_f[:], R=["state_f"], W=["state_b"])
            fpool.put(dts[b])
        for c in range(16):
            bpool.put(xact[c])
        if last:
            for half in range(2):
                pt, pk = PS()
                for c4 in range(4):
                    c = half * 4 + c4
                    tr(pt[:, c4 * 128:(c4 + 1) * 128], state_f[:, c * 128:(c + 1) * 128], ident_f[:],
                       R=["state_f", "ident_f"], W=[pk])
                cp("act", tokout[:, half * 512:(half + 1) * 512], pt[:, :], R=[pk], W=["tokout"])
            dstt = O["ssm_p"][l, sq] if kind == "p" else O["ssm_s"][l, 0]
            dma_sp(dstt.rearrange("(c h) p n -> (h p) c n", h=2), tokout[:, :].rearrange("q (c n) -> q c n", n=128),
                   R=["tokout"], W=[("o_ssm", l, sq)])
        if kind == 's' and cfg.stop <= 9:
            raise _Stop()
        yz = []
        for g in range(2):
            ws, wk = WS.next("z%d" % g)
            for c4 in range(4):
                c = g * 4 + c4
                pt, pk = PS()
                win_chunk(ws, wk, KC, 512, c4 * 128, 128, pt[:, 0:T], pk)
                zb, zk = bpool.get()
                act(zb[:, 0:T], pt[:, 0:T], AF.Silu, R=[pk], W=[zk])
                tt("dve", zb[:, 0:T], zb[:, 0:T], yT[:, c, 0:T], ALU.mult, R=[zk, "yT"], W=[zk])
                yz.append((zb, zk))
        for g in range(SG):
            fb, fk = sumsq([(yz[2 * g + i][0][:, 0:T], yz[2 * g + i][1]) for i in range(2)], T, 256)
            for i in range(2):
                c = 2 * g + i
                stt("dve", yz[c][0][:, 0:T], yz[c][0][:, 0:T], pcol("g_ssm", l, c), fb[:, 0:T], ALU.mult, ALU.mult,
                    R=[yz[c][1], fk, "params"], W=[yz[c][1]])
            fpool.put((fb, fk))

        if kind == 's' and cfg.stop <= 10:
            raise _Stop()
        if kind == os.environ.get("DBGKIND", "s") and DBG:
            def dbg_dump(dst, src, np_, shp, R):
                n = 1
                for v in shp:
                    n *= v
                view = tokout[0:np_, 0:n]
                if len(shp) == 2:
                    view = view.rearrange("p (a b) -> p a b", b=shp[1])
                cp("dve", view, src, R=R, W=["tokout"])
                dma_sp(dst, view, R=["tokout"], W=["dbgout"])
            if "Kc0" in DBG:
                dbg_dump(DBG["Kc0"], Kc[:, 0, 0:1024], QD, [1024], [("Kc", 0)])
            if "Vc0" in DBG:
                dbg_dump(DBG["Vc0"], Vc[:, 0:8, 0, 0:64], 128, [8, 64], [("Vc", kb) for kb in range(8)])
            if "Qp0" in DBG:
                dbg_dump(DBG["Qp0"], Qp[:, 0, 0:T], QD, [T], [("Qp", 0)])
            for c in range(4):
                if "cT" in DBG:
                    dbg_dump(DBG["cT"][c], cT[c][0][:, 0:T], 128, [T], [cT[c][1]])
            if "On" in DBG:
                for hh_ in range(0, NH, 4):
                    dbg_dump(DBG["On"][:, hh_:hh_ + 4, :], On[:, hh_:hh_ + 4, 0:T], 64, [4, T], [("On", h) for h in range(NH)])
            for c in range(8):
                if "yz" in DBG:
                    dbg_dump(DBG["yz"][c], yz[c][0][:, 0:T], 128, [T], [yz[c][1]])
        merged = []
        for cb in range(2):
            macc = [fpool.get() for _ in range(4)]
            for i in range(3):
                ws, wk = WS.next("gate%d_%d" % (i, cb))
                gts = []
                for c4 in range(4):
                    pt, pk = PS()
                    win_chunk(ws, wk, KC, 512, c4 * 128, 128, pt[:, 0:T], pk)
                    gb, gk = bpool.get()
                    act(gb[:, 0:T], pt[:, 0:T], AF.Sigmoid, R=[pk], W=[gk])
                    gts.append((gb, gk))
                wsm, wkm = WS.next(("mconv%d", "mattn%d", "mssm%d")[i] % cb)
                for c4 in range(4):
                    cs_ = slice(c4 * 128, (c4 + 1) * 128)
                    pt, pk = PS()
                    if i == 0:
                        wvc = wsm[:, 0:4 * 512].rearrange("p (kc n) -> p kc n", n=512)
                        for kc in range(4):
                            mm(pt[:, 0:T], wvc[:, kc, cs_], cT[kc][0][:, 0:T], kc == 0, kc == 3, R=[wkm, cT[kc][1]], W=[pk])
                    elif i == 1:
                        wva = wsm[0:64, 0:NH * 512].rearrange("p (h n) -> p h n", n=512)
                        for h in range(NH):
                            mm(pt[:, 0:T], wva[:, h, cs_], On[:, h, 0:T], h == 0, h == NH - 1, R=[wkm, ("On", h)], W=[pk])
                    else:
                        wvs = wsm[:, 0:8 * 512].rearrange("p (kc n) -> p kc n", n=512)
                        for kc in range(8):
                            mm(pt[:, 0:T], wvs[:, kc, cs_], yz[kc][0][:, 0:T], kc == 0, kc == 7, R=[wkm, yz[kc][1]], W=[pk])
                    m0, mk0 = macc[c4]
                    if i == 0:
                        tt("dve", m0[:, 0:T], pt[:, 0:T], gts[c4][0][:, 0:T], ALU.mult, R=[pk, gts[c4][1]], W=[mk0])
                    else:
                        m1, mk1 = fpool.get()
                        tt("dve", m1[:, 0:T], pt[:, 0:T], gts[c4][0][:, 0:T], ALU.mult, R=[pk, gts[c4][1]], W=[mk1])
                        if i == 1:
                            tt("dve", m0[:, 0:T], m0[:, 0:T], m1[:, 0:T], ALU.add, R=[mk0, mk1], W=[mk0])
                        else:
                            mb_, mbk = bpool.get()
                            tt("dve", mb_[:, 0:T], m0[:, 0:T], m1[:, 0:T], ALU.add, R=[mk0, mk1], W=[mbk])
                            merged.append((mb_, mbk))
                        fpool.put((m1, mk1))
                    bpool.put(gts[c4])
            for c4 in range(4):
                fpool.put(macc[c4])
        for c in range(4):
            bpool.put(cT[c])
        for c in range(8):
            bpool.put(yz[c])
        for c in range(KC):
            bpool.put(hT[c])

        def proj_norm_residual(chunks_fn, gname):
            outs = []
            for c in range(KC):
                pt, pk = chunks_fn(c)
                ob, ok = fpool.get()
                cp("act" if c % 2 else "dve", ob[:, 0:T], pt[:, 0:T], R=[pk], W=[ok])
                outs.append((ob, ok))
            fb, fk = sumsq([(o[0][:, 0:T], o[1]) for o in outs], T, D)
            for c in range(KC):
                ob, ok = outs[c]
                stt("dve", ob[:, 0:T], ob[:, 0:T], pcol(gname, l, c), fb[:, 0:T], ALU.mult, ALU.mult,
                    R=[ok, fk, "params"], W=[ok])
                tt("dve", xb[:, c, 0:T], xb[:, c, 0:T], ob[:, 0:T], ALU.add, R=[xk, ok], W=[xk])
                fpool.put((ob, ok))
            fpool.put((fb, fk))

        wo = {}

        def out_chunk(c):
            cb = c // 4
            if c % 4 == 0:
                wo["w"] = WS.next("wout%d" % cb)
            ws, wk = wo["w"]
            wv_ = ws[:, 0:KC * 512].rearrange("p (kc n) -> p kc n", n=512)
            pt, pk = PS()
            for kc in range(KC):
                mm(pt[:, 0:T], wv_[:, kc, (c % 4) * 128:(c % 4 + 1) * 128], merged[kc][0][:, 0:T], kc == 0, kc == KC - 1,
                   R=[wk, merged[kc][1]], W=[pk])
            return pt, pk
        proj_norm_residual(out_chunk, "g_post_mix")
        for c in range(KC):
            bpool.put(merged[c])

        fb, fk = sumsq([(xb[:, c, 0:T], xk) for c in range(KC)], T, D)
        h2 = [bpool.get() for _ in range(KC)]
        for c in range(KC):
            stt("dve", h2[c][0][:, 0:T], xb[:, c, 0:T], pcol("g_pre_ffn", l, c), fb[:, 0:T], ALU.mult, ALU.mult,
                R=[xk, fk, "params"], W=[h2[c][1]])
        fpool.put((fb, fk))
        actc = [None] * FC
        for g in range(11):
            ws, wk = WS.next("gu%d" % g)
            wv_ = ws[:, 0:KC * 512].rearrange("p (kc n) -> p kc n", n=512)
            for c4 in range(4):
                cc = g * 4 + c4
                pt, pk = PS()
                for kc in range(KC):
                    mm(pt[:, 0:T], wv_[:, kc, c4 * 128:(c4 + 1) * 128], h2[kc][0][:, 0:T], kc == 0, kc == KC - 1,
                       R=[wk, h2[kc][1]], W=[pk])
                if cc < FC:
                    ab, ak = bpool.get()
                    act(ab[:, 0:T], pt[:, 0:T], AF.Silu, R=[pk], W=[ak])
                    actc[cc] = (ab, ak)
                else:
                    ab, ak = actc[cc - FC]
                    tt("dve", ab[:, 0:T], pt[:, 0:T], ab[:, 0:T], ALU.mult, R=[pk, ak], W=[ak])
        for c in range(KC):
            bpool.put(h2[c])
        dn = {}

        def down_chunks(cb):
            pts = [PS() for _ in range(4)]
            for kg in range(3):
                nk = 8 if kg < 2 else 6
                ws, wk = WS.next("dn%d_%d" % (cb, kg))
                wv_ = ws[:, 0:nk * 512].rearrange("p (kc n) -> p kc n", n=512)
                for c4 in range(4):
                    for kc in range(nk):
                        kk = kg * 8 + kc
                        mm(pts[c4][0][:, 0:T], wv_[:, kc, c4 * 128:(c4 + 1) * 128], actc[kk][0][:, 0:T],
                           kk == 0, kk == FC - 1, R=[wk, actc[kk][1]], W=[pts[c4][1]])
            return pts

        def down_chunk(c):
            if c % 4 == 0:
                dn["p"] = down_chunks(c // 4)
            return dn["p"][c % 4]
        proj_norm_residual(down_chunk, "g_post_ffn")
        for c in range(FC):
            bpool.put(actc[c])

        if l == cfg.depth - 1:
            for b in range(NB):
                nb = min(128, T - b * 128)
                for half in range(2):
                    pt, pk = PS()
                    for c4 in range(4):
                        c = half * 4 + c4
                        tr(pt[0:nb, c4 * 128:(c4 + 1) * 128], xb[:, c, b * 128:b * 128 + nb], ident_f[:], R=[xk, "ident_f"], W=[pk])
                    cp("act", tokout[0:nb, half * 512:(half + 1) * 512], pt[0:nb, :], R=[pk], W=["tokout"])
                dst = (O["y_prompt"][sq, pos0 + b * 128:pos0 + b * 128 + nb, :] if kind == "p" else O["y_sample"][0, 0:nb, :])
                dma_sp(dst, tokout[0:nb, :], R=["tokout"], W=[("o_y", sq, j, b)])
        else:
            p0 = j * TT if kind == "p" else 0
            dma_sp(xres[sq, :, :, p0:p0 + T], xb[:, :, 0:T], R=[xk], W=[("xres", sq, j)])

    try:
        for ti in range(len(tiles)):
            do_tile(ti)
        assert WS.taken == len(WS.plan), (WS.taken, len(WS.plan))
    except _Stop:
        pass


_OUT_ORDER = ["y_prompt", "y_sample", "lat_p", "kr_p", "conv_p", "sconv_p", "ssm_p",
              "lat_s", "kr_s", "conv_s", "sconv_s", "ssm_s"]


def kernel(**inputs):
    n = 8
    cfg = Cfg()
    nc = build(cfg)
    consts = host_consts(cfg)
    arr = {k: np.ascontiguousarray(np.asarray(v, dtype=np.float32)) for k, v in inputs.items()}
    in_maps = []
    for c in range(n):
        m = {}
        m["x_prompt"] = np.ascontiguousarray(arr["x_prompt"][2 * c:2 * c + 2])
        m["x_sample"] = np.ascontiguousarray(arr["x_sample"][c:c + 1])
        for k in ("cache_mla_latent", "cache_mla_rope", "state_conv", "state_ssm_conv", "state_ssm"):
            m[k] = np.ascontiguousarray(arr[k][:, c:c + 1])
        for k in WNAMES:
            m[k] = arr[k]
        m["c_ident"] = consts["ident"]
        m["c_tri"] = consts["tri"]
        m["c_lstrict"] = consts["lstrict"]
        m["c_rcos"] = consts["rcos"]
        m["c_rsin"] = consts["rsin"]
        in_maps.append(m)
    res = run_bass_kernel_spmd(nc, in_maps, core_ids=list(range(n)))
    r = res.results
    outs = []
    for k in _OUT_ORDER:
        ax = 0 if k in ("y_prompt", "y_sample") else 1
        outs.append(np.concatenate([np.asarray(r[c][k], dtype=np.float32) for c in range(n)], axis=ax))
    return tuple(outs)
```

```python
import math
import os
from contextlib import ExitStack

import numpy as np
import concourse.bass as bass
import concourse.mybir as mybir
from concourse.bass_utils import run_bass_kernel_spmd

F32 = mybir.dt.float32
BF16 = mybir.dt.bfloat16
AF = mybir.ActivationFunctionType
ALU = mybir.AluOpType

D = 1024
KC = 8
D_CONV = 512
CONV_K = 31
NH = 8
Q_LORA = 384
KV_LORA = 256
QK_NOPE = 64
QK_ROPE = 32
V_DIM = 64
QD = QK_NOPE + QK_ROPE
ATTN_SCALE = QD ** -0.5
D_INNER = 1024
HP = 64
SH = 16
SG = 4
NS = 128
XBC = 2048
D_FF = 2816
FC = 22
EPS = 1e-6
PAST = 1024
DEC_T = 16
IN_W = 7856
C_GLU, C_Q, C_KV, C_KR, C_Z, C_XBC, C_DT, C_GATE = 0, 1024, 1408, 1664, 1696, 2720, 4768, 4784

NSLOT = 4
SLOT_ELEMS = 4096


class Op:
    __slots__ = ("eng", "meth", "kw", "deps", "is_dma", "chan", "cnt", "signal", "sigcnt", "waits")

    def __init__(self, eng, meth, kw, deps, is_dma=False, chan=None, cnt=0):
        self.eng = eng
        self.meth = meth
        self.kw = kw
        self.deps = deps
        self.is_dma = is_dma
        self.chan = chan
        self.cnt = cnt
        self.signal = False
        self.sigcnt = 0
        self.waits = []


class Prog:
    ENGS = ("pe", "act", "dve", "pool", "sp")

    def __init__(self, nc, stack):
        self.nc = nc
        self.stack = stack
        self.ops = []
        self.st = {}
        self.chan_last = {}
        self.chan_cnt = {}
        self.n_sb = 0

    def sb(self, name, shape, dtype):
        return self.stack.enter_context(self.nc.sbuf_tensor(name, list(shape), dtype))

    def psum(self, name, shape, dtype):
        return self.stack.enter_context(self.nc.psum_tensor(name, list(shape), dtype))

    def _track(self, idx, eng, R, W, is_dma):
        deps = set()
        for k in R:
            s = self.st.get(k)
            if s is not None and s[0] is not None:
                deps.add(s[0])
            if s is not None and isinstance(k, str) and k.startswith("ps"):
                for e2, v in s[1]:
                    if e2 != eng:
                        deps.add(v)
        for k in W:
            s = self.st.get(k)
            if s is not None:
                if s[0] is not None:
                    deps.add(s[0])
                deps.update(v for _, v in s[1])
                deps.update(s[2])
        for k in R:
            s = self.st.setdefault(k, [None, [], []])
            if is_dma:
                s[2].append(idx)
            else:
                s[1].append((eng, idx))
        for k in W:
            self.st[k] = [idx, [], []]
        deps.discard(idx)
        return deps

    def op(self, eng, meth, R=(), W=(), **kw):
        idx = len(self.ops)
        deps = self._track(idx, eng, R, W, False)
        self.ops.append(Op(eng, meth, kw, deps))
        return idx

    def dma(self, q, out, in_, R, W, chan, **kw):
        idx = len(self.ops)
        deps = self._track(idx, q, R, W, True)
        if chan in self.chan_last:
            deps.add(self.chan_last[chan])
        self.chan_last[chan] = idx
        self.chan_cnt[chan] = self.chan_cnt.get(chan, 0) + 1
        kw = dict(kw)
        kw["out"] = out
        kw["in_"] = in_
        self.ops.append(Op(q, "dma_start", kw, deps, True, chan, self.chan_cnt[chan]))
        return idx

    def _dur(self, o):
        kw = o.kw
        try:
            if o.is_dma:
                return 0.2 if o.eng == "sp" else 1.2
            if o.eng == "pe":
                mv = kw["rhs"] if o.meth == "matmul" else kw["in_"]
                n = mv.free_size()
                f = 4.0 if mv.dtype == F32 else 1.0
                return (max(64, n) * f) / 2.0 + 12.0
            outap = kw.get("out", kw.get("ap"))
            fd = outap.free_size()
            if o.eng == "act":
                return (224 + fd) / 1.2
            if o.eng == "pool":
                return (150 + fd * 2.0) / 1.2
            f = 1.0
            ins = [kw[k] for k in ("in0", "in1", "in_") if k in kw and hasattr(kw[k], "dtype")]
            if ins and all(a.dtype == BF16 for a in ins) and outap.dtype == BF16:
                f = 0.5
            if o.meth == "memset":
                f = 0.5
            return (120 + fd * f) / 0.96
        except Exception:
            return 500.0

    def _dma_bytes(self, o):
        try:
            a = o.kw["in_"]
            n = 1
            for d in a.shape:
                n *= d
            return n * (4 if a.dtype == F32 else 2)
        except Exception:
            return 1 << 20

    def schedule(self):
        import heapq
        ops = self.ops
        n = len(ops)
        dur = [self._dur(o) for o in ops]
        tail = [0.0] * n
        for i, o in enumerate(ops):
            if o.is_dma:
                tail[i] = 2000.0 + self._dma_bytes(o) / 300.0
        keep = os.environ.get("SCHED_KEEP", "").split(",")
        lastop = {}
        kr_ = os.environ.get("KEEP_RANGE", "")
        klo, khi = (int(v) for v in kr_.split(":")) if kr_ else (0, 1 << 60)
        for i, o in enumerate(ops):
            if o.eng in keep and klo <= i < khi:
                if o.eng in lastop:
                    o.deps.add(lastop[o.eng])
                lastop[o.eng] = i
        succ = [[] for _ in range(n)]
        indeg = [0] * n
        for i, o in enumerate(ops):
            for j in o.deps:
                succ[j].append(i)
            indeg[i] = len(o.deps)
        bl = [0.0] * n
        for i in range(n - 1, -1, -1):
            m = 0.0
            for k in succ[i]:
                if bl[k] > m:
                    m = bl[k]
            bl[i] = m + dur[i] + tail[i]
        SYNC = 120.0
        ready = [0.0] * n
        rsrc = [-1] * n
        gaps = {}
        fin = [0.0] * n
        pend = {e: [] for e in self.ENGS}
        avail = {e: [] for e in self.ENGS}
        ft = {e: 0.0 for e in self.ENGS}
        for i in range(n):
            if indeg[i] == 0:
                heapq.heappush(pend[ops[i].eng], (0.0, i))
        dma_free = 0.0
        acls = [({"Exp": "E", "Ln": "E", "Sigmoid": "S", "Silu": "U"}.get(str(o.kw.get("func", "")).split(".")[-1])
                 if o.eng == "act" else None) for o in ops]
        act_cur = [None]
        order = {e: [] for e in self.ENGS}
        self.gorder = []
        done = 0
        while done < n:
            best_e = None
            best_t = None
            for e in self.ENGS:
                pe_, av_ = pend[e], avail[e]
                while pe_ and pe_[0][0] <= ft[e]:
                    r, i = heapq.heappop(pe_)
                    heapq.heappush(av_, (-bl[i], i))
                if av_:
                    t = ft[e]
                elif pe_:
                    t = pe_[0][0]
                else:
                    continue
                if best_t is None or t < best_t:
                    best_t = t
                    best_e = e
            e = best_e
            if avail[e]:
                if e == "act" and len(avail[e]) > 1:
                    cand = [heapq.heappop(avail[e]) for _ in range(min(6, len(avail[e])))]
                    pick = 0
                    if acls[cand[0][1]] not in (None, act_cur[0]):
                        for ci, (nb_, ii) in enumerate(cand):
                            if acls[ii] in (None, act_cur[0]) and -nb_ >= -cand[0][0] - 3000.0:
                                pick = ci
                                break
                    _, i = cand.pop(pick)
                    for c_ in cand:
                        heapq.heappush(avail[e], c_)
                else:
                    _, i = heapq.heappop(avail[e])
                start = ft[e]
            else:
                r, i = heapq.heappop(pend[e])
                start = r
            o = ops[i]
            if e == "act" and acls[i] is not None and acls[i] != act_cur[0]:
                act_cur[0] = acls[i]
                start += 1300.0
            if start > ft[e] + 1e-9 and rsrc[i] >= 0:
                b_ = ops[rsrc[i]]
                kk_ = (e, b_.eng, b_.meth, "dma" if b_.is_dma else str(b_.kw.get("func", "")).split(".")[-1])
                gaps[kk_] = gaps.get(kk_, 0.0) + (start - ft[e])
            ft[e] = start + dur[i]
            if o.is_dma:
                xs = max(ft[e], dma_free)
                xf = xs + self._dma_bytes(o) / 300.0
                dma_free = xf
                fin[i] = xf + 2000.0
            else:
                fin[i] = ft[e]
            order[e].append(i)
            self.gorder.append(i)
            done += 1
            for k in succ[i]:
                rt = fin[i] + (0.0 if (ops[k].eng == e and e == "pe" and not o.is_dma) else SYNC)
                if rt > ready[k]:
                    ready[k] = rt
                    rsrc[k] = i
                indeg[k] -= 1
                if indeg[k] == 0:
                    heapq.heappush(pend[ops[k].eng], (ready[k], k))
        self.est_ns = max(fin) if n else 0.0
        busy = {e: 0.0 for e in self.ENGS}
        cntm = {}
        for i, o in enumerate(ops):
            busy[o.eng] += dur[i]
            kk = (o.eng, o.meth)
            c_ = cntm.setdefault(kk, [0, 0.0])
            c_[0] += 1
            c_[1] += dur[i]
        self.est_info = (busy, dma_free, cntm)
        self.gaps = gaps
        return order

    def finalize(self, resched=True):
        nc = self.nc
        ops = self.ops
        if resched:
            order = self.schedule()
        else:
            order = {e: [i for i, o in enumerate(ops) if o.eng == e] for e in self.ENGS}
        cls_prev = None
        nsw = 0
        for i in order["act"]:
            f_ = str(ops[i].kw.get("func", "")).split(".")[-1]
            c_ = {"Exp": "E", "Ln": "E", "Sigmoid": "S", "Silu": "U"}.get(f_)
            if c_ and c_ != cls_prev:
                nsw += 1
                cls_prev = c_
        self.n_tblsw = nsw
        pos = [0] * len(ops)
        for e in self.ENGS:
            for p_, i in enumerate(order[e]):
                pos[i] = p_
        gorder = getattr(self, "gorder", None) if resched else None
        if not gorder:
            gorder = list(range(len(ops)))
        waited = {e: {} for e in self.ENGS}
        know = [None] * len(ops)
        nw = 0
        for i in gorder:
            o = ops[i]
            e = o.eng
            cur = waited[e]
            need = {}
            for j in o.deps:
                oj = ops[j]
                if oj.is_dma:
                    src = ("c", oj.chan)
                    val = oj.cnt
                else:
                    if oj.eng == "pe" and e == "pe" and not o.is_dma:
                        assert pos[j] < pos[i]
                        continue
                    src = oj.eng
                    val = pos[j]
                    if src == e:
                        assert pos[j] < pos[i]
                if src not in need or val > need[src][0]:
                    need[src] = (val, j)
            waits = []
            for src, (val, j) in sorted(need.items(), key=lambda t: -len(know[t[1][1]] or ())):
                if cur.get(src, -1) >= val:
                    continue
                waits.append((src, j))
                cur[src] = val
                kj = know[j]
                if kj:
                    for s2, v2 in kj.items():
                        if cur.get(s2, -1) < v2:
                            cur[s2] = v2
                if not ops[j].is_dma:
                    ops[j].signal = True
            o.waits = waits
            nw += len(waits)
            k_ = dict(cur)
            if not o.is_dma:
                k_[e] = max(k_.get(e, -1), pos[i])
            know[i] = k_
        self.n_waits = nw
        for e in self.ENGS:
            c = 0
            for i in order[e]:
                o = ops[i]
                if (not o.is_dma) and o.signal:
                    c += 1
                    o.sigcnt = c
        esem = {e: self.stack.enter_context(nc.semaphore("s_" + e)) for e in ("pe", "act", "dve", "pool")}
        csem = {c: self.stack.enter_context(nc.semaphore("c_%d" % i)) for i, c in enumerate(self.chan_cnt)}
        chan_final = dict(self.chan_cnt)
        block = self.stack.enter_context(nc.Block())

        def make_body(e):
            def body(eng):
                for i in order[e]:
                    o = ops[i]
                    for src, j in o.waits:
                        if isinstance(src, tuple):
                            eng.wait_ge(csem[src[1]], 16 * ops[j].cnt)
                        else:
                            eng.wait_ge(esem[src], ops[j].sigcnt)
                    ins = getattr(eng, o.meth)(**o.kw)
                    if o.is_dma:
                        ins.then_inc(csem[o.chan], 16)
                    elif o.signal:
                        ins.then_inc(esem[e], 1)
                if e == "sp":
                    for c, n in chan_final.items():
                        eng.wait_ge(csem[c], 16 * n)
            return body

        block.tensor(make_body("pe"))
        block.scalar(make_body("act"))
        block.vector(make_body("dve"))
        block.gpsimd(make_body("pool"))
        block.sync(make_body("sp"))


class Pool:
    def __init__(self, P, name, n, shape, dtype):
        self.free = []
        for i in range(n):
            t = P.sb("%s%d" % (name, i), shape, dtype)
            self.free.append((t, "%s%d" % (name, i)))
        self.n = n

    def get(self):
        assert self.free, "pool exhausted"
        return self.free.pop(0)

    def put(self, b):
        self.free.append(b)


def bc_ap(t, part, dims, offset=0):
    pstep = t[:].ap[0][0]
    return bass.AP(t, offset, [[pstep, part]] + [list(d) for d in dims])


class _Stop(Exception):
    pass


class Cfg:
    def __init__(self, nseq=2, seq=2048, depth=4, sample=True, tt=256, dbg=(), stop=99):
        self.nseq = nseq
        self.seq = seq
        self.depth = depth
        self.sample = sample
        self.tt = tt
        self.dbg = tuple(dbg)
        self.stop = stop


def host_consts(cfg):
    c = {}
    c["ident"] = np.eye(128, dtype=np.float32)
    k = np.arange(128)
    c["tri"] = (k[:, None] <= k[None, :]).astype(np.float32)
    c["lstrict"] = (k[:, None] > k[None, :]).astype(np.float32)
    half = QK_ROPE // 2
    inv_freq = (10000.0 ** (-np.arange(half, dtype=np.float32) / half)).astype(np.float32)
    pos = np.concatenate([np.arange(cfg.seq), PAST + np.arange(DEC_T)]).astype(np.float32)
    ang = pos[None, :] * inv_freq[:, None]
    cos = np.ones((QD, pos.shape[0]), np.float32)
    sin = np.zeros((QD, pos.shape[0]), np.float32)
    cos[64:80] = np.cos(ang)
    cos[80:96] = np.cos(ang)
    sin[64:80] = np.sin(ang)
    sin[80:96] = np.sin(ang)
    c["rcos"] = cos
    c["rsin"] = sin
    return c


WNAMES = ["g_pre_mix", "g_post_mix", "g_pre_ffn", "g_post_ffn", "w_in", "conv_w", "conv_b", "conv_ln_g",
          "conv_ln_b", "g_q", "w_uq", "g_kv", "w_ukv", "ssm_conv_w", "ssm_conv_b", "dt_bias", "a_log",
          "d_skip", "g_ssm", "w_mix_out", "w_out", "w_gate_up", "w_down"]


def build(cfg):
    nc = bass.Bass("TRN2", target_bir_lowering=False)
    L, NSEQ, SEQ, TT = cfg.depth, cfg.nseq, cfg.seq, cfg.tt
    NT = SEQ // TT

    def din(name, shape):
        return nc.dram_tensor(name, list(shape), F32, kind="ExternalInput").ap()

    def dout(name, shape):
        return nc.dram_tensor(name, list(shape), F32, kind="ExternalOutput").ap()

    I = {}
    I["x_prompt"] = din("x_prompt", [NSEQ, SEQ, D])
    I["x_sample"] = din("x_sample", [1, DEC_T, D])
    I["cache_mla_latent"] = din("cache_mla_latent", [L, 1, PAST, KV_LORA])
    I["cache_mla_rope"] = din("cache_mla_rope", [L, 1, PAST, QK_ROPE])
    I["state_conv"] = din("state_conv", [L, 1, CONV_K - 1, D_CONV])
    I["state_ssm_conv"] = din("state_ssm_conv", [L, 1, 3, XBC])
    I["state_ssm"] = din("state_ssm", [L, 1, SH, HP, NS])
    wshapes = {"g_pre_mix": [L, D], "g_post_mix": [L, D], "g_pre_ffn": [L, D], "g_post_ffn": [L, D],
               "w_in": [L, D, IN_W], "conv_w": [L, CONV_K, D_CONV], "conv_b": [L, D_CONV],
               "conv_ln_g": [L, D_CONV], "conv_ln_b": [L, D_CONV], "g_q": [L, Q_LORA],
               "w_uq": [L, Q_LORA, NH * QD], "g_kv": [L, KV_LORA], "w_ukv": [L, KV_LORA, NH * 128],
               "ssm_conv_w": [L, 4, XBC], "ssm_conv_b": [L, XBC], "dt_bias": [L, SH], "a_log": [L, SH],
               "d_skip": [L, SH], "g_ssm": [L, D_INNER], "w_mix_out": [L, 2048, D], "w_out": [L, D, D],
               "w_gate_up": [L, D, 2 * D_FF], "w_down": [L, D_FF, D]}
    for n in WNAMES:
        I[n] = din(n, wshapes[n])
    NPOS = SEQ + DEC_T
    I["c_ident"] = din("c_ident", [128, 128])
    I["c_tri"] = din("c_tri", [128, 128])
    I["c_lstrict"] = din("c_lstrict", [128, 128])
    I["c_rcos"] = din("c_rcos", [QD, NPOS])
    I["c_rsin"] = din("c_rsin", [QD, NPOS])

    O = {}
    O["y_prompt"] = dout("y_prompt", [NSEQ, SEQ, D])
    O["y_sample"] = dout("y_sample", [1, DEC_T, D])
    O["lat_p"] = dout("lat_p", [L, NSEQ, SEQ, KV_LORA])
    O["kr_p"] = dout("kr_p", [L, NSEQ, SEQ, QK_ROPE])
    O["conv_p"] = dout("conv_p", [L, NSEQ, CONV_K - 1, D_CONV])
    O["sconv_p"] = dout("sconv_p", [L, NSEQ, 3, XBC])
    O["ssm_p"] = dout("ssm_p", [L, NSEQ, SH, HP, NS])
    O["lat_s"] = dout("lat_s", [L, 1, DEC_T, KV_LORA])
    O["kr_s"] = dout("kr_s", [L, 1, DEC_T, QK_ROPE])
    O["conv_s"] = dout("conv_s", [L, 1, CONV_K - 1, D_CONV])
    O["sconv_s"] = dout("sconv_s", [L, 1, 3, XBC])
    O["ssm_s"] = dout("ssm_s", [L, 1, SH, HP, NS])
    DBG = {}
    for (nm, shp) in cfg.dbg:
        DBG[nm] = dout("dbg_" + nm, shp)

    xres = nc.dram_tensor("xres", [NSEQ + 1, 128, KC, SEQ], F32, kind="Internal").ap()
    dgd = nc.dram_tensor("wsc", [L, 64, 128, SLOT_ELEMS], BF16, kind="Internal").ap()

    stack = ExitStack()
    with stack:
        P = Prog(nc, stack)
        emit_program(P, cfg, I, O, DBG, xres, dgd)
        print('sbuf_free', nc.sbuf_bytes_remaining, flush=True)
        P.finalize(resched=not os.environ.get('NO_RESCHED'))
        print('tblsw', getattr(P, 'n_tblsw', -1), 'waits', getattr(P, 'n_waits', -1), 'ops', len(P.ops), 'est_ms', getattr(P, 'est_ns', 0) / 1e6, {k: round(v / 1e6, 2) for k, v in P.est_info[0].items()}, 'dma_ms', P.est_info[1] / 1e6, {k: (v[0], round(v[1] / 1e6, 2)) for k, v in P.est_info[2].items()}, flush=True)
        print('gaps(ms)', sorted([(k, round(v / 1e6, 2)) for k, v in P.gaps.items() if v > 2e5], key=lambda t: -t[1]), flush=True)
    return nc


def emit_program(P, cfg, I, O, DBG, xres, dgd):
    nc = P.nc
    L, NSEQ, SEQ, TT = cfg.depth, cfg.nseq, cfg.seq, cfg.tt
    NT = SEQ // TT
    NPOS = SEQ + DEC_T

    ident_f = P.sb("ident_f", [128, 128], F32)
    ident_b = P.sb("ident_b", [128, 128], BF16)
    ones_f = P.sb("ones_f", [128, 128], F32)
    ones_b = P.sb("ones_b", [128, 128], BF16)
    tri_f = P.sb("tri_f", [128, 128], F32)
    tri_b = P.sb("tri_b", [128, 128], BF16)
    lst_b = P.sb("lst_b", [128, 128], BF16)
    NPC = L * (4 * 8 + 3 * 4 + 3 + 2 + 16 + 8 + 124 + 64)
    params = P.sb("params", [128, NPC], F32)
    bcp = P.sb("bcp", [128, 3, L * SH], F32)
    slots = [P.sb("wslot%d" % i, [128, SLOT_ELEMS], BF16) for i in range(NSLOT)]
    wuq_rot = P.sb("wuq_rot", [128, 3, NH * QD], BF16)
    wkr_pad = P.sb("wkr_pad", [128, KC, QD], BF16)
    wkr_rot = P.sb("wkr_rot", [128, KC, QD], BF16)
    KMAX = max(SEQ, PAST + DEC_T)
    NKB = (KMAX + 127) // 128
    Kc = P.sb("Kc", [QD, NH, KMAX], BF16)
    Vc = P.sb("Vc", [128, NKB, NH, 66], BF16)
    state_f = P.sb("state_f", [128, 1024], F32)
    state_b = P.sb("state_b", [128, 1024], BF16)
    xbuf = [P.sb("xT%d" % i, [128, KC, TT], F32) for i in range(2)]
    rope_t = P.sb("rope_t", [QD, 2, TT], F32)
    a_buf = P.sb("a_buf", [128, 4, 30 + TT], BF16)
    xhist = P.sb("xhist", [128, 16, 3], BF16)
    rawb = [P.sb("rawb%d" % i, [128, 4, 3 + TT], BF16) for i in range(2)]
    Qp = P.sb("Qp", [QD, NH, TT], BF16)
    On = P.sb("On", [64, NH, TT], BF16)
    yT = P.sb("yT", [128, KC, TT], BF16)
    NPB = 3
    pbufs = [P.sb("pbuf%d" % i, [128, 2, TT], BF16) for i in range(NPB)]
    bpool = Pool(P, "bch", 34, [128, TT], BF16)
    fpool = Pool(P, "fch", 11, [128, TT], F32)
    R_t = P.sb("R_t", [128, SH, 128], BF16)
    eD_t = P.sb("eD_t", [128, SH, 128], BF16)
    Gm_t = P.sb("Gm_t", [128, SG, 128], BF16)
    xdt_t = P.sb("xdt_t", [128, SH, HP], BF16)
    xdte_t = P.sb("xdte_t", [128, SH, HP], BF16)
    xD_t = P.sb("xD_t", [128, 1024], BF16)
    Btm_t = P.sb("Btm_t", [128, 512], BF16)
    yt_t = P.sb("yt_t", [128, 1024], F32)
    ybf_t = P.sb("ybf_t", [128, 1024], BF16)
    sm = P.sb("ssd_small", [128, 8, SH], F32)
    tokout = P.sb("tokout", [128, 1024], F32)
    tokin = P.sb("tokin", [128, 1024], F32)
    latT = P.sb("latT", [128, 1, 256], F32)
    krT = P.sb("krT", [128, 1, 32], F32)
    qt1 = P.sb("qt1", [QD, TT], F32)
    qt2 = P.sb("qt2", [QD, TT], F32)
    krf = P.sb("krf", [QD, TT], F32)
    rden = P.sb("rden", [128, TT], F32)
    bcs = P.sb("bcs", [64, TT], F32)
    st30 = P.sb("st30", [32, 512], F32)
    kr96 = P.sb("kr96", [128, QD], F32)

    psf = [P.psum("psf%d" % i, [128, 512], F32) for i in range(8)]
    ps_rr = [0]

    def PS(excl=()):
        while True:
            i = ps_rr[0] % 8
            ps_rr[0] += 1
            if ("psf%d" % i) not in excl:
                return psf[i], "psf%d" % i
    psb_rr = [0]

    def PSB():
        t_, k_ = PS()
        return t_[:].bitcast(BF16), k_

    def mm(out, lhsT, rhs, start, stop, R, W):
        P.op("pe", "matmul", R=R, W=W, out=out, lhsT=lhsT, rhs=rhs, start=start, stop=stop)

    def tr(out, in_, ident, R, W):
        P.op("pe", "transpose", R=R, W=W, out=out, in_=in_, identity=ident)

    def act(out, in_, func, R, W, **kw):
        P.op("act", "activation", R=R, W=W, out=out, in_=in_, func=func, **kw)

    def tt(eng, out, in0, in1, op, R, W):
        P.op(eng, "tensor_tensor", R=R, W=W, out=out, in0=in0, in1=in1, op=op)

    def ts(eng, out, in0, s1, s2, op0, op1, R, W):
        if op1 is None:
            P.op(eng, "tensor_scalar", R=R, W=W, out=out, in0=in0, scalar1=s1, scalar2=None, op0=op0)
        else:
            P.op(eng, "tensor_scalar", R=R, W=W, out=out, in0=in0, scalar1=s1, scalar2=s2, op0=op0, op1=op1)

    def stt(eng, out, in0, scalar, in1, op0, op1, R, W):
        P.op(eng, "scalar_tensor_tensor", R=R, W=W, out=out, in0=in0, scalar=scalar, in1=in1, op0=op0, op1=op1)

    def cp(eng, out, in_, R, W):
        if eng == "act":
            P.op("act", "activation", R=R, W=W, out=out, in_=in_, func=AF.Copy)
        else:
            P.op(eng, "tensor_copy", R=R, W=W, out=out, in_=in_)

    def memset(eng, ap, val, W):
        P.op(eng, "memset", R=(), W=W, ap=ap, constant=val)

    dma_rr = [0]

    def dma_sp(out, in_, R, W, chan=None, **kw):
        if chan is None:
            chan = "g%d" % (dma_rr[0] % 8)
            dma_rr[0] += 1
        P.dma("sp", out, in_, R, W, chan, **kw)

    def dbg(name, ap):
        if name in DBG:
            dma_sp(DBG[name], ap, R=["*dbg"], W=["dbg_" + name])

    pc = {}
    off = 0
    for nm, n in (("g_pre_mix", 8), ("g_post_mix", 8), ("g_pre_ffn", 8), ("g_post_ffn", 8), ("conv_b", 4),
                  ("conv_ln_g", 4), ("conv_ln_b", 4), ("g_q", 3), ("g_kv", 2), ("ssm_conv_b", 16),
                  ("g_ssm", 8), ("conv_w", 124), ("ssm_conv_w", 64)):
        pc[nm] = (off, n)
        off += L * n
    assert off == NPC

    def pcol(nm, l, c):
        o, n = pc[nm]
        return params[:, o + l * n + c: o + l * n + c + 1]

    dma_sp(ident_f[:], I["c_ident"], R=[], W=["ident_f"])
    cp("dve", ident_b[:], ident_f[:], R=["ident_f"], W=["ident_b"])
    memset("dve", ones_f[:], 1.0, W=["ones_f"])
    memset("dve", ones_b[:], 1.0, W=["ones_b"])
    dma_sp(tri_f[:], I["c_tri"], R=[], W=["tri_f"])
    cp("dve", tri_b[:], tri_f[:], R=["tri_f"], W=["tri_b"])
    dma_sp(tokin[:, 0:128], I["c_lstrict"], R=[], W=["tokin"])
    cp("dve", lst_b[:], tokin[:, 0:128], R=["tokin"], W=["lst_b"])
    memset("dve", wuq_rot[:], 0.0, W=["wuq_rot"])
    memset("dve", wkr_pad[:], 0.0, W=["wkr_pad"])
    memset("dve", wkr_rot[:], 0.0, W=["wkr_rot"])
    memset("dve", Vc[:], 1.0, W=[("Vc", kb_) for kb_ in range(NKB)])
    memset("dve", kr96[:], 0.0, W=["kr96"])

    def load_param_cols(nm, rows_ap, nrows):
        o, n = pc[nm]
        r0 = 0
        while r0 < nrows:
            nr = min(128, nrows - r0)
            dma_sp(tokin[0:nr, 0:128], rows_ap[r0:r0 + nr, :], R=[], W=["tokin"])
            pt, pk = PS()
            tr(pt[:, 0:nr], tokin[0:nr, 0:128], ident_f[0:nr, 0:nr], R=["tokin", "ident_f"], W=[pk])
            cp("dve", params[:, o + r0:o + r0 + nr], pt[:, 0:nr], R=[pk], W=["params"])
            r0 += nr

    for nm in ("g_pre_mix", "g_post_mix", "g_pre_ffn", "g_post_ffn", "conv_b", "conv_ln_g", "conv_ln_b",
               "g_q", "g_kv", "ssm_conv_b", "g_ssm"):
        n = pc[nm][1]
        load_param_cols(nm, I[nm].rearrange("l (c p) -> (l c) p", p=128), L * n)
    load_param_cols("conv_w", I["conv_w"].rearrange("l k (c p) -> (l k c) p", p=128), L * 124)
    load_param_cols("ssm_conv_w", I["ssm_conv_w"].rearrange("l k (c p) -> (l k c) p", p=128), L * 64)
    for i, nm in enumerate(("dt_bias", "a_log", "d_skip")):
        src = I[nm].rearrange("l h -> (l h)").partition_broadcast(128)
        dma_sp(bcp[:, i, :], src, R=[], W=["bcp"])
    act(bcp[:, 1, :], bcp[:, 1, :], AF.Exp, R=["bcp"], W=["bcp"])
    ts("dve", bcp[:, 1, :], bcp[:, 1, :], -1.0, None, ALU.mult, None, R=["bcp"], W=["bcp"])

    sl_rr = [0]

    def w_cols(nm, l, r0, nk, c0, ncol):
        def fn(slot, sk, s):
            src = I[nm][l, r0:r0 + 128 * nk, c0:c0 + ncol].rearrange("(kc p) n -> p kc n", p=128)
            dst = slot[:, 0:nk * ncol].rearrange("p (kc n) -> p kc n", n=ncol)
            P.dma("pool", dst, src, R=[], W=[sk], chan="w%d" % s)

        def d2d(dview, wkey, chan, gate):
            src = I[nm][l, r0:r0 + 128 * nk, c0:c0 + ncol].rearrange("(kc p) n -> p kc n", p=128)
            P.dma("pool", dview[0:128, 0:nk * ncol].rearrange("p (kc n) -> p kc n", n=ncol), src, R=gate, W=[wkey], chan=chan)
        fn.d2d = d2d
        return fn, 128, nk * ncol

    def w_attn(l, c0, ncol):
        def fn(slot, sk, s):
            src = I["w_mix_out"][l, 512:1024, c0:c0 + ncol].rearrange("(h p) n -> p h n", p=64)
            dst = slot[0:64, 0:NH * ncol].rearrange("p (h n) -> p h n", n=ncol)
            P.dma("pool", dst, src, R=[], W=[sk], chan="w%d" % s)

        def d2d(dview, wkey, chan, gate):
            src = I["w_mix_out"][l, 512:1024, c0:c0 + ncol].rearrange("(h p) n -> p h n", p=64)
            P.dma("pool", dview[0:64, 0:NH * ncol].rearrange("p (h n) -> p h n", n=ncol), src, R=gate, W=[wkey], chan=chan)
        fn.d2d = d2d
        return fn, 64, NH * ncol

    def w_diag(l, g):
        def fn(slot, sk, s):
            sv = slot[:].rearrange("p (i q) -> p i q", q=128)
            if g < 4:
                for k in range(CONV_K):
                    ts("dve", sv[:, k, :], ident_b[:], pcol("conv_w", l, k * 4 + g), None, ALU.mult, None,
                       R=["ident_b", "params"], W=[sk])
            else:
                for c4 in range(4):
                    c = (g - 4) * 4 + c4
                    for k in range(4):
                        ts("dve", sv[:, c4 * 4 + k, :], ident_b[:], pcol("ssm_conv_w", l, k * 16 + c), None,
                           ALU.mult, None, R=["ident_b", "params"], W=[sk])
        return fn, 128, (CONV_K if g < 4 else 16) * 128

    def layer_groups(l):
        G = []
        G.append(("glu_v", w_cols("w_in", l, 0, KC, C_GLU, 512)))
        G.append(("glu_g", w_cols("w_in", l, 0, KC, C_GLU + 512, 512)))
        for c in range(4):
            G.append(("cdiag%d" % c, w_diag(l, c)))
        G.append(("q", w_cols("w_in", l, 0, KC, C_Q, Q_LORA)))
        G.append(("kvkr", w_cols("w_in", l, 0, KC, C_KV, KV_LORA + QK_ROPE)))
        G.append(("wuq", w_cols("w_uq", l, 0, 3, 0, NH * QD)))
        G.append(("wukv", w_cols("w_ukv", l, 0, 2, 0, NH * 128)))
        for g in range(4):
            G.append(("xbc%d" % g, w_cols("w_in", l, 0, KC, C_XBC + 512 * g, 512)))
            G.append(("xdiag%d" % g, w_diag(l, 4 + g)))
        G.append(("dt", w_cols("w_in", l, 0, KC, C_DT, SH)))
        for g in range(2):
            G.append(("z%d" % g, w_cols("w_in", l, 0, KC, C_Z + 512 * g, 512)))
        for cb in range(2):
            G.append(("gate0_%d" % cb, w_cols("w_in", l, 0, KC, C_GATE + 512 * cb, 512)))
            G.append(("mconv%d" % cb, w_cols("w_mix_out", l, 0, 4, 512 * cb, 512)))
            G.append(("gate1_%d" % cb, w_cols("w_in", l, 0, KC, C_GATE + 1024 + 512 * cb, 512)))
            G.append(("mattn%d" % cb, w_attn(l, 512 * cb, 512)))
            G.append(("gate2_%d" % cb, w_cols("w_in", l, 0, KC, C_GATE + 2048 + 512 * cb, 512)))
            G.append(("mssm%d" % cb, w_cols("w_mix_out", l, 1024, 8, 512 * cb, 512)))
        for cb in range(2):
            G.append(("wout%d" % cb, w_cols("w_out", l, 0, KC, 512 * cb, 512)))
        for g in range(11):
            G.append(("gu%d" % g, w_cols("w_gate_up", l, 0, KC, 512 * g, 512)))
        for cb in range(2):
            for kg in range(3):
                nk = 8 if kg < 2 else 6
                G.append(("dn%d_%d" % (cb, kg), w_cols("w_down", l, 1024 * kg, nk, 512 * cb, 512)))
        return G

    LG = [layer_groups(l) for l in range(L)]
    pc_rr = [0]

    def precast_layer(l, gate):
        for gi, (name, (fill, np_, ne)) in enumerate(LG[l]):
            if hasattr(fill, "d2d"):
                fill.d2d(dgd[l, gi], ("wsc", l, gi), "pc%d" % (pc_rr[0] % 8), gate)
                pc_rr[0] += 1

    assert len(LG[0]) <= 64
    for l in range(L):
        if l >= 1:
            continue
        precast_layer(l, [])

    class WStream:
        def __init__(self):
            self.plan = []
            self.issued = 0
            self.taken = 0
            self.rr = sl_rr[0]

        def add(self, name, fn):
            self.plan.append((name, fn))

        def _issue(self):
            name, fn = self.plan[self.issued]
            s = (self.rr + self.issued) % NSLOT
            fn(slots[s], "wslot%d" % s, s)
            self.issued += 1

        def next(self, name):
            assert self.plan[self.taken][0] == name, (self.plan[self.taken][0], name)
            while self.issued < min(len(self.plan), self.taken + NSLOT - 1):
                self._issue()
            s = (self.rr + self.taken) % NSLOT
            self.taken += 1
            return slots[s], "wslot%d" % s

    WS = WStream()

    def w_scr(l, gi, np_, ne):
        def fn(slot, sk, s):
            P.dma("sp", slot[0:np_, 0:ne], dgd[l, gi, 0:np_, 0:ne], R=[("wsc", l, gi)], W=[sk], chan="w%d" % s)
        return fn

    def build_store(l, gi, fill, np_, ne):
        def fn(slot, sk, s):
            fill(slot, sk, s)
            dma_sp(dgd[l, gi, 0:np_, 0:ne], slot[0:np_, 0:ne], R=[sk], W=[("wsc", l, gi)], chan="ws%d" % s)
        return fn

    def plan_tile(l, direct=False, first_of_layer=False):
        for gi, (name, (fill, np_, ne)) in enumerate(LG[l]):
            if not hasattr(fill, "d2d"):
                WS.add(name, build_store(l, gi, fill, np_, ne) if first_of_layer else w_scr(l, gi, np_, ne))
            else:
                WS.add(name, fill if direct else w_scr(l, gi, np_, ne))

    tiles = []
    for l in range(L):
        for sq in range(NSEQ):
            for j in range(NT):
                if not os.environ.get('SKIP_PROMPT'):
                    tiles.append((l, "p", sq, j))
        if cfg.sample:
            tiles.append((l, "s", NSEQ, 0))
    for ti_, (l, kind, sq, j) in enumerate(tiles):
        plan_tile(l, direct=(ti_ == 0), first_of_layer=(ti_ == 0 or tiles[ti_ - 1][0] != l))

    def rstd_from(ps_ap, invd, T, fk, fb, R):
        act(fb[:, 0:T], ps_ap, AF.Ln, R=R, W=[fk], scale=invd, bias=EPS)
        act(fb[:, 0:T], fb[:, 0:T], AF.Exp, R=[fk], W=[fk], scale=-0.5)

    def sumsq(chunks, T, dim):
        pt, pk = PS()
        n = len(chunks)
        for i, (ap, k) in enumerate(chunks):
            sb_, sk_ = bpool.get()
            act(sb_[:, 0:T], ap, AF.Square, R=[k], W=[sk_])
            mm(pt[:, 0:T], ones_b[:], sb_[:, 0:T], i == 0, i == n - 1, R=["ones_b", sk_], W=[pk])
            bpool.put((sb_, sk_))
        fb, fk = fpool.get()
        rstd_from(pt[:, 0:T], 1.0 / dim, T, fk, fb, [pk])
        return fb, fk

    seqst = {"nkeys": 0}
    xnext = [0]

    def load_x_tile(tile, xb, xk):
        (l, kind, sq, j) = tile
        T = TT if kind == "p" else DEC_T
        if l == 0:
            src = I["x_prompt"][sq] if kind == "p" else I["x_sample"][0]
            p0 = j * TT if kind == "p" else 0
            b0 = 0
            while b0 < T:
                nb = min(128, T - b0)
                dma_sp(tokin[0:nb, :], src[p0 + b0:p0 + b0 + nb, :], R=[], W=["tokin"])
                for half in range(2):
                    pt, pk = PS()
                    for c4 in range(4):
                        c = half * 4 + c4
                        tr(pt[:, c4 * 128:c4 * 128 + nb], tokin[0:nb, c * 128:(c + 1) * 128], ident_f[0:nb, 0:nb],
                           R=["tokin", "ident_f"], W=[pk])
                    cp("act", xb[:, half * 4:half * 4 + 4, b0:b0 + nb],
                       pt[:, :].rearrange("p (c t) -> p c t", t=128)[:, :, 0:nb], R=[pk], W=[xk])
                b0 += nb
        else:
            p0 = j * TT if kind == "p" else 0
            dma_sp(xb[:, :, 0:T], xres[sq, :, :, p0:p0 + T], R=[("xres", sq, j)], W=[xk])

    def do_tile(ti):
        tile = tiles[ti]
        (l, kind, sq, j) = tile
        T = TT if kind == "p" else DEC_T
        pos0 = j * TT if kind == "p" else 0
        tcol0 = pos0 if kind == "p" else SEQ
        key0 = pos0 if kind == "p" else PAST
        first = (j == 0)
        last = (kind == "s") or (j == NT - 1)
        NB = (T + 127) // 128
        xb = xbuf[ti % 2]
        xk = "xT%d" % (ti % 2)
        load_x_tile(tile, xb, xk)
        lfirst = (ti == 0 or tiles[ti - 1][0] != l)
        dma_sp(rope_t[:, 0, 0:T], I["c_rcos"][:, tcol0:tcol0 + T], R=[], W=["rope_t"] + ([("gate", l)] if lfirst else []))
        if lfirst and l + 1 < L:
            precast_layer(l + 1, [("gate", l)])
        dma_sp(rope_t[:, 1, 0:T], I["c_rsin"][:, tcol0:tcol0 + T], R=[], W=["rope_t"])
        COS = rope_t[:, 0, 0:T]
        SIN = rope_t[:, 1, 0:T]

        if first:
            if kind == "p":
                memset("dve", a_buf[:, :, 0:30], 0.0, W=["a_buf"])
                memset("dve", xhist[:], 0.0, W=["xhist"])
                memset("dve", state_f[:], 0.0, W=["state_f"])
                memset("dve", state_b[:], 0.0, W=["state_b"])
            else:
                dma_sp(st30[0:30, :], I["state_conv"][l, 0], R=[], W=["st30"])
                pt, pk = PS()
                for c in range(4):
                    tr(pt[:, c * 32:c * 32 + 30], st30[0:30, c * 128:(c + 1) * 128], ident_f[0:30, 0:30],
                       R=["st30", "ident_f"], W=[pk])
                cp("act", a_buf[:, :, 0:30], pt[:, 0:128].rearrange("p (c t) -> p c t", t=32)[:, :, 0:30],
                   R=[pk], W=["a_buf"])
                if int(os.environ.get('INITSTOP', '99')) <= 1:
                    raise _Stop()
                if not os.environ.get("SKIP_D2D"):
                    dma_sp(O["conv_s"][l, 0, 0:14, :], I["state_conv"][l, 0, 16:30, :], R=[], W=[("o_conv_s", l)])
                if int(os.environ.get('INITSTOP', '99')) <= 2:
                    raise _Stop()
                dma_sp(tokin[0:3, 0:1024], I["state_ssm_conv"][l, 0, :, 0:1024], R=[], W=["tokin"])
                dma_sp(tokout[0:3, 0:1024], I["state_ssm_conv"][l, 0, :, 1024:2048], R=[], W=["tokout"])
                pt, pk = PS()
                for c in range(16):
                    srcb = tokin if c < 8 else tokout
                    tr(pt[:, c * 4:c * 4 + 3], srcb[0:3, (c % 8) * 128:(c % 8 + 1) * 128], ident_f[0:3, 0:3],
                       R=["tokin", "tokout", "ident_f"], W=[pk])
                cp("act", xhist[:], pt[:, 0:64].rearrange("p (c t) -> p c t", t=4)[:, :, 0:3], R=[pk], W=["xhist"])
                if int(os.environ.get('INITSTOP', '99')) <= 3:
                    raise _Stop()
                src = I["state_ssm"][l, 0].rearrange("(c h) p n -> (h p) c n", h=2)
                dma_sp(tokin[:, :].rearrange("q (c n) -> q c n", n=128), src, R=[], W=["tokin"])
                for half in range(2):
                    pt, pk = PS()
                    for c4 in range(4):
                        c = half * 4 + c4
                        tr(pt[:, c4 * 128:(c4 + 1) * 128], tokin[:, c * 128:(c + 1) * 128], ident_f[:],
                           R=["tokin", "ident_f"], W=[pk])
                    cp("act", state_f[:, half * 512:(half + 1) * 512], pt[:, :], R=[pk], W=["state_f"])
                cp("dve", state_b[:], state_f[:], R=["state_f"], W=["state_b"])
                if int(os.environ.get('INITSTOP', '99')) <= 4:
                    raise _Stop()
                latc = [bpool.get() for _ in range(2 * (PAST // TT))]
                for kb in range(PAST // 128):
                    dma_sp(tokin[:, 0:256], I["cache_mla_latent"][l, 0, kb * 128:(kb + 1) * 128, :], R=[], W=["tokin"])
                    if not os.environ.get("SKIP_KRC"):
                        dma_sp(kr96[:, 64:96], I["cache_mla_rope"][l, 0, kb * 128:(kb + 1) * 128, :], R=[], W=["kr96"])
                    pt, pk = PS()
                    for c in range(2):
                        tr(pt[:, c * 128:(c + 1) * 128], tokin[:, c * 128:(c + 1) * 128], ident_f[:],
                           R=["tokin", "ident_f"], W=[pk])
                    if not os.environ.get("SKIP_KRT"):
                        tr(pt[0:QD, 256:384], kr96[:, :], ident_f[:], R=["kr96", "ident_f"], W=[pk])
                    u = (kb * 128) // TT
                    o = (kb * 128) % TT
                    for c in range(2):
                        lb, lk = latc[c * (PAST // TT) + u]
                        cp("act", lb[:, o:o + 128], pt[:, c * 128:(c + 1) * 128], R=[pk], W=[lk])
                    cp("act", krf[64:96, 0:128], pt[64:96, 256:384], R=[pk], W=["krf"])
                    for h in range(NH):
                        cp("dve" if h % 2 else "act", Kc[64:96, h, kb * 128:(kb + 1) * 128], krf[64:96, 0:128],
                           R=["krf"], W=[("Kc", h)])
                seqst["latc"] = latc
            seqst["nkeys"] = 0 if kind == "p" else PAST

        if kind == 's' and cfg.stop <= 0:
            raise _Stop()
        fb, fk = sumsq([(xb[:, c, 0:T], xk) for c in range(KC)], T, D)
        hT = [bpool.get() for _ in range(KC)]
        for c in range(KC):
            stt("dve", hT[c][0][:, 0:T], xb[:, c, 0:T], pcol("g_pre_mix", l, c), fb[:, 0:T], ALU.mult, ALU.mult,
                R=[xk, fk, "params"], W=[hT[c][1]])
        fpool.put((fb, fk))
        hk = [h[1] for h in hT]

        def win_chunk(ws, wk, nk, ncol, j0, M, pt_ap, pk):
            wv = ws[:, 0:nk * ncol].rearrange("p (kc n) -> p kc n", n=ncol)
            for kc in range(nk):
                mm(pt_ap, wv[:, kc, j0:j0 + M], hT[kc][0][:, 0:T], kc == 0, kc == nk - 1, R=[wk, hk[kc]], W=[pk])

        def tokmajor_tail(ws, wk, ncol, ntok, pt, pk, c0=0, n=None):
            n = ncol if n is None else n
            wv = ws[:, 0:KC * ncol].rearrange("p (kc n) -> p kc n", n=ncol)
            for kc in range(KC):
                mm(pt[0:ntok, 0:n], hT[kc][0][:, T - ntok:T], wv[:, kc, c0:c0 + n], kc == 0, kc == KC - 1,
                   R=[wk, hk[kc]], W=[pk])

        if kind == 's' and cfg.stop <= 1:
            raise _Stop()
        ntail = min(30, T)
        ws, wk = WS.next("glu_v")
        vps = []
        for c in range(4):
            pt, pk = PS()
            win_chunk(ws, wk, KC, 512, c * 128, 128, pt[:, 0:T], pk)
            vps.append((pt, pk))
        if last:
            ptv, pkv = PS()
            tokmajor_tail(ws, wk, 512, ntail, ptv, pkv)
            cp("act", st30[0:ntail, :], ptv[0:ntail, :], R=[pkv], W=["st30"])
        vfs = []
        for c in range(4):
            vb, vk = fpool.get()
            cp("act" if c % 2 else "dve", vb[:, 0:T], vps[c][0][:, 0:T], R=[vps[c][1]], W=[vk])
            vfs.append((vb, vk))
        ws, wk = WS.next("glu_g")
        for c in range(4):
            pt, pk = PS()
            win_chunk(ws, wk, KC, 512, c * 128, 128, pt[:, 0:T], pk)
            sb_, sk_ = fpool.get()
            act(sb_[:, 0:T], pt[:, 0:T], AF.Sigmoid, R=[pk], W=[sk_])
            tt("dve", a_buf[:, c, 30:30 + T], vfs[c][0][:, 0:T], sb_[:, 0:T], ALU.mult, R=[vfs[c][1], sk_], W=["a_buf"])
            fpool.put((sb_, sk_))
            fpool.put(vfs[c])
        if last:
            ptg, pkg = PS()
            tokmajor_tail(ws, wk, 512, ntail, ptg, pkg)
            act(tokout[0:ntail, 0:512], ptg[0:ntail, :], AF.Sigmoid, R=[pkg], W=["tokout"])
            tt("dve", tokout[0:ntail, 0:512], tokout[0:ntail, 0:512], st30[0:ntail, :], ALU.mult,
               R=["tokout", "st30"], W=["tokout"])
            if kind == "p":
                dma_sp(O["conv_p"][l, sq], tokout[0:30, 0:512], R=["tokout"], W=[("o_conv_p", l, sq)])
            else:
                dma_sp(O["conv_s"][l, 0, 14:30, :], tokout[0:16, 0:512], R=["tokout"], W=[("o_conv_s2", l)])
        cos_ = []
        for c in range(4):
            ws, wk = WS.next("cdiag%d" % c)
            dv = ws[:].rearrange("p (i q) -> p i q", q=128)
            pt, pk = PS()
            for k in range(CONV_K):
                mm(pt[:, 0:T], dv[:, k, :], a_buf[:, c, k:k + T], k == 0, k == CONV_K - 1, R=[wk, "a_buf"], W=[pk])
            cb_, ck_ = fpool.get()
            act(cb_[:, 0:T], pt[:, 0:T], AF.Identity, R=[pk, "params"], W=[ck_], bias=pcol("conv_b", l, c))
            cos_.append((cb_, ck_))
        if not last:
            for c in range(4):
                cp("pool", a_buf[:, c, 0:30], a_buf[:, c, T:T + 30], R=["a_buf"], W=["a_buf"])
        pt1, pk1 = PS()
        pt2, pk2 = PS()
        for c in range(4):
            b1, k1 = bpool.get()
            b2, k2 = bpool.get()
            cp("dve", b1[:, 0:T], cos_[c][0][:, 0:T], R=[cos_[c][1]], W=[k1])
            act(b2[:, 0:T], cos_[c][0][:, 0:T], AF.Square, R=[cos_[c][1]], W=[k2])
            mm(pt1[:, 0:T], ones_b[:], b1[:, 0:T], c == 0, c == 3, R=["ones_b", k1], W=[pk1])
            mm(pt2[:, 0:T], ones_b[:], b2[:, 0:T], c == 0, c == 3, R=["ones_b", k2], W=[pk2])
            bpool.put((b1, k1))
            bpool.put((b2, k2))
        mb, mk = fpool.get()
        rb, rk = fpool.get()
        ts("dve", mb[:, 0:T], pt1[:, 0:T], 1.0 / D_CONV, None, ALU.mult, None, R=[pk1], W=[mk])
        tt("dve", rb[:, 0:T], mb[:, 0:T], mb[:, 0:T], ALU.mult, R=[mk], W=[rk])
        stt("dve", rb[:, 0:T], pt2[:, 0:T], 1.0 / D_CONV, rb[:, 0:T], ALU.mult, ALU.subtract, R=[pk2, rk], W=[rk])
        act(rb[:, 0:T], rb[:, 0:T], AF.Ln, R=[rk], W=[rk], bias=EPS)
        act(rb[:, 0:T], rb[:, 0:T], AF.Exp, R=[rk], W=[rk], scale=-0.5)
        cT = []
        for c in range(4):
            cb_, ck_ = cos_[c]
            tt("dve", cb_[:, 0:T], cb_[:, 0:T], mb[:, 0:T], ALU.subtract, R=[ck_, mk], W=[ck_])
            tt("dve", cb_[:, 0:T], cb_[:, 0:T], rb[:, 0:T], ALU.mult, R=[ck_, rk], W=[ck_])
            ob, ok = bpool.get()
            act(ob[:, 0:T], cb_[:, 0:T], AF.Silu, R=[ck_, "params"], W=[ok],
                scale=pcol("conv_ln_g", l, c), bias=pcol("conv_ln_b", l, c))
            cT.append((ob, ok))
            fpool.put((cb_, ck_))
        fpool.put((mb, mk))
        fpool.put((rb, rk))

        if kind == 's' and cfg.stop <= 2:
            raise _Stop()
        ws, wk = WS.next("q")
        uq = []
        for c in range(3):
            pt, pk = PS()
            win_chunk(ws, wk, KC, Q_LORA, c * 128, 128, pt[:, 0:T], pk)
            ub, uk = fpool.get()
            cp("act", ub[:, 0:T], pt[:, 0:T], R=[pk], W=[uk])
            uq.append((ub, uk))
        fb, fk = sumsq([(u[0][:, 0:T], u[1]) for u in uq], T, Q_LORA)
        qn = []
        for c in range(3):
            qb, qk = bpool.get()
            stt("dve", qb[:, 0:T], uq[c][0][:, 0:T], pcol("g_q", l, c), fb[:, 0:T], ALU.mult, ALU.mult,
                R=[uq[c][1], fk, "params"], W=[qk])
            qn.append((qb, qk))
            fpool.put(uq[c])
        fpool.put((fb, fk))

        ws, wk = WS.next("kvkr")
        NCKV = KV_LORA + QK_ROPE
        wv = ws[:, 0:KC * NCKV].rearrange("p (kc n) -> p kc n", n=NCKV)
        cp("act", wkr_pad[:, :, 64:96], wv[:, :, 256:288], R=[wk], W=["wkr_pad"])
        cp("act", wkr_rot[:, :, 80:96], wv[:, :, 256:272], R=[wk], W=["wkr_rot"])
        ts("dve", wkr_rot[:, :, 64:80], wv[:, :, 272:288], -1.0, None, ALU.mult, None, R=[wk], W=["wkr_rot"])
        ukv = []
        for c in range(2):
            pt, pk = PS()
            win_chunk(ws, wk, KC, NCKV, c * 128, 128, pt[:, 0:T], pk)
            ub, uk = fpool.get()
            cp("act", ub[:, 0:T], pt[:, 0:T], R=[pk], W=[uk])
            ukv.append((ub, uk))
        ptA, pkA = PS()
        ptB, pkB = PS()
        for kc in range(KC):
            mm(ptA[0:QD, 0:T], wkr_pad[:, kc, :], hT[kc][0][:, 0:T], kc == 0, kc == KC - 1, R=["wkr_pad", hk[kc]], W=[pkA])
        for kc in range(KC):
            mm(ptB[0:QD, 0:T], wkr_rot[:, kc, :], hT[kc][0][:, 0:T], kc == 0, kc == KC - 1, R=["wkr_rot", hk[kc]], W=[pkB])
        tt("dve", qt1[64:96, 0:T], ptB[64:96, 0:T], SIN[64:96, :], ALU.mult, R=[pkB, "rope_t"], W=["qt1"])
        tt("dve", qt2[64:96, 0:T], ptA[64:96, 0:T], COS[64:96, :], ALU.mult, R=[pkA, "rope_t"], W=["qt2"])
        tt("dve", krf[64:96, 0:T], qt1[64:96, 0:T], qt2[64:96, 0:T], ALU.add, R=["qt1", "qt2"], W=["krf"])
        nk0 = seqst["nkeys"]
        for h in range(NH):
            cp("act" if h % 2 else "dve", Kc[64:96, h, nk0:nk0 + T], krf[64:96, 0:T], R=["krf"], W=[("Kc", h)])
        for b in range(NB):
            nb = min(128, T - b * 128)
            pt, pk = PS()
            tr(pt[0:nb, 0:32], krf[64:96, b * 128:b * 128 + nb], ident_f[64:96, 64:96], R=["krf", "ident_f"], W=[pk])
            cp("act", krT[0:nb, 0, :], pt[0:nb, 0:32], R=[pk], W=["krT0"])
            dst = (O["kr_p"][l, sq, pos0 + b * 128:pos0 + b * 128 + nb, :] if kind == "p"
                   else O["kr_s"][l, 0, 0:nb, :])
            dma_sp(dst, krT[0:nb, 0, :], R=["krT0"], W=[("o_kr", l, sq, j, b)])
        fb, fk = sumsq([(u[0][:, 0:T], u[1]) for u in ukv], T, KV_LORA)
        latb = []
        for c in range(2):
            ub, uk = ukv[c]
            stt("dve", ub[:, 0:T], ub[:, 0:T], pcol("g_kv", l, c), fb[:, 0:T], ALU.mult, ALU.mult,
                R=[uk, fk, "params"], W=[uk])
            lb, lk = bpool.get()
            cp("pool", lb[:, 0:T], ub[:, 0:T], R=[uk], W=[lk])
            latb.append((lb, lk))
        fpool.put((fb, fk))
        for b in range(NB):
            nb = min(128, T - b * 128)
            pt, pk = PS()
            for c in range(2):
                tr(pt[0:nb, c * 128:(c + 1) * 128], ukv[c][0][:, b * 128:b * 128 + nb], ident_f[:],
                   R=[ukv[c][1], "ident_f"], W=[pk])
            cp("act", latT[0:nb, 0, :], pt[0:nb, 0:256], R=[pk], W=["latT0"])
            dst = (O["lat_p"][l, sq, pos0 + b * 128:pos0 + b * 128 + nb, :] if kind == "p"
                   else O["lat_s"][l, 0, 0:nb, :])
            dma_sp(dst, latT[0:nb, 0, :], R=["latT0"], W=[("o_lat", l, sq, j, b)])
        for c in range(2):
            fpool.put(ukv[c])

        if kind == 's' and cfg.stop <= 3:
            raise _Stop()
        ws_q, wk_q = WS.next("wuq")
        wq = ws_q[:, 0:3 * NH * QD].rearrange("p (kc n) -> p kc n", n=NH * QD)
        wq4 = ws_q[:, 0:3 * NH * QD].rearrange("p (kc h d) -> p kc h d", h=NH, d=QD)
        wr4 = wuq_rot[:].rearrange("p kc (h d) -> p kc h d", d=QD)
        for kc in range(3):
            cp("act", wr4[:, kc, :, 80:96], wq4[:, kc, :, 64:80], R=[wk_q], W=["wuq_rot"])
            ts("dve", wr4[:, kc, :, 64:80], wq4[:, kc, :, 80:96], -1.0, None, ALU.mult, None, R=[wk_q], W=["wuq_rot"])
        for h in range(NH):
            ptA, pkA = PS()
            ptB, pkB = PS()
            for kc in range(3):
                mm(ptA[0:QD, 0:T], wq[:, kc, h * QD:(h + 1) * QD], qn[kc][0][:, 0:T], kc == 0, kc == 2,
                   R=[wk_q, qn[kc][1]], W=[pkA])
            for kc in range(3):
                mm(ptB[0:QD, 0:T], wuq_rot[:, kc, h * QD:(h + 1) * QD], qn[kc][0][:, 0:T], kc == 0, kc == 2,
                   R=["wuq_rot", qn[kc][1]], W=[pkB])
            tt("dve", qt1[:, 0:T], ptB[0:QD, 0:T], SIN, ALU.mult, R=[pkB, "rope_t"], W=["qt1"])
            tt("dve", qt2[:, 0:T], ptA[0:QD, 0:T], COS, ALU.mult, R=[pkA, "rope_t"], W=["qt2"])
            tt("dve", Qp[:, h, 0:T], qt1[:, 0:T], qt2[:, 0:T], ALU.add, R=["qt1", "qt2"], W=[("Qp", h)])
        for c in range(3):
            bpool.put(qn[c])

        if kind == 's' and cfg.stop <= 4:
            raise _Stop()
        ws, wk = WS.next("wukv")
        wkv = ws[:, 0:2 * NH * 128].rearrange("p (kc n) -> p kc n", n=NH * 128)

        def kv_gen(lat_chunks, Tn, kbase):
            for h in range(0, NH, 2):
                pt, pk = PS()
                for a_ in range(2):
                    for kc in range(2):
                        mm(pt[0:64, a_ * TT:a_ * TT + Tn], wkv[:, kc, (h + a_) * 128:(h + a_) * 128 + 64], lat_chunks[kc][0],
                           kc == 0, kc == 1, R=[wk, lat_chunks[kc][1]], W=[pk])
                cp("act" if (h // 2) % 2 else "dve", Kc[0:64, h:h + 2, kbase:kbase + Tn],
                   pt[0:64, 0:2 * TT].rearrange("p (a t) -> p a t", t=TT)[:, :, 0:Tn], R=[pk], W=[("Kc", h), ("Kc", h + 1)])
            nblk = (Tn + 127) // 128
            for b in range(nblk):
                nb = min(128, Tn - b * 128)
                kb = (kbase + b * 128) // 128
                for half in range(2):
                    pt, pk = PS()
                    for kc in range(2):
                        mm(pt[0:nb, :], lat_chunks[kc][0][:, b * 128:b * 128 + nb], wkv[:, kc, half * 512:(half + 1) * 512],
                           kc == 0, kc == 1, R=[wk, lat_chunks[kc][1]], W=[pk])
                    cp("act" if half else "dve", Vc[0:nb, kb, half * 4:half * 4 + 4, 0:64],
                       pt[0:nb, :].rearrange("p (h d) -> p h d", d=128)[:, :, 64:128], R=[pk], W=[("Vc", kb)])

        if kind == "s":
            latc = seqst["latc"]
            nu = PAST // TT
            for u in range(nu):
                kv_gen([(latc[c * nu + u][0][:, 0:TT], latc[c * nu + u][1]) for c in range(2)], TT, u * TT)
            for b_ in latc:
                bpool.put(b_)
        kv_gen([(latb[c][0][:, 0:T], latb[c][1]) for c in range(2)], T, nk0)
        for c in range(2):
            bpool.put(latb[c])
        nkeys = nk0 + T
        seqst["nkeys"] = nkeys

        if kind == 's' and cfg.stop <= 5:
            raise _Stop()
        nkb = (nkeys + 127) // 128
        pb_rr = 0
        for hp in range(NH // 2):
            hs = (2 * hp, 2 * hp + 1)
            po0, pok0 = PS()
            po1, pok1 = PS(excl=(pok0,))
            pos_ = ((po0, pok0), (po1, pok1))
            for kb in range(nkb):
                ks = kb * 128
                nkk = min(128, nkeys - ks)
                if kind == "p" and ks >= key0:
                    q0 = ks - key0
                else:
                    q0 = 0
                nq = T - q0
                pt, pk = PS(excl=(pok0, pok1))
                for a_, h in enumerate(hs):
                    mm(pt[0:nkk, a_ * TT:a_ * TT + nq], Kc[:, h, ks:ks + nkk], Qp[:, h, q0:T], True, True,
                       R=[("Kc", h), ("Qp", h)], W=[pk])
                pb = pbufs[pb_rr % NPB]
                pbk = "pbuf%d" % (pb_rr % NPB)
                pb_rr += 1
                act(pb[0:nkk, :, 0:nq], pt[0:nkk, 0:2 * TT].rearrange("p (a t) -> p a t", t=TT)[:, :, 0:nq], AF.Exp,
                    R=[pk], W=[pbk], scale=ATTN_SCALE)
                if kind == "p" and ks >= key0:
                    memset("dve", pb[64:128, :, 0:64], 0.0, W=[pbk])
                for a_, h in enumerate(hs):
                    mm(pos_[a_][0][0:65, q0:T], Vc[0:nkk, kb, h, 0:65], pb[0:nkk, a_, 0:nq], kb == 0, kb == nkb - 1,
                       R=[("Vc", kb), pbk], W=[pos_[a_][1]])
            for a_, h in enumerate(hs):
                po, pok = pos_[a_]
                P.op("dve", "reciprocal", R=[pok], W=["rden"], out=rden[64:65, 0:T], in_=po[64:65, 0:T])
                pb2, pbk2 = PS(excl=(pok0, pok1))
                mm(pb2[0:64, 0:T], ones_f[64:65, 0:64], rden[64:65, 0:T], True, True, R=["ones_f", "rden"], W=[pbk2])
                cp("act", bcs[:, 0:T], pb2[0:64, 0:T], R=[pbk2], W=["bcs"])
                tt("dve", On[:, h, 0:T], po[0:64, 0:T], bcs[:, 0:T], ALU.mult, R=[pok, "bcs"], W=[("On", h)])

        if kind == 's' and cfg.stop <= 6:
            raise _Stop()
        xact = [None] * 16
        for g in range(4):
            ws, wk = WS.next("xbc%d" % g)
            wsd, wkd = WS.next("xdiag%d" % g)
            dvx = wsd[:].rearrange("p (i q) -> p i q", q=128)
            rb_ = rawb[g % 2]
            rk_ = "rawb%d" % (g % 2)
            cp("dve", rb_[:, :, 0:3], xhist[:, g * 4:g * 4 + 4, :], R=["xhist"], W=[rk_])
            for c4 in (0, 2):
                pt, pk = PS()
                win_chunk(ws, wk, KC, 512, c4 * 128, 128, pt[:, 0:T], pk)
                win_chunk(ws, wk, KC, 512, (c4 + 1) * 128, 128, pt[:, TT:TT + T], pk)
                cp("act" if c4 else "dve", rb_[:, c4:c4 + 2, 3:3 + T],
                   pt[:, 0:2 * TT].rearrange("p (a t) -> p a t", t=TT)[:, :, 0:T], R=[pk], W=[rk_])
            if last:
                pt, pk = PS()
                tokmajor_tail(ws, wk, 512, 3, pt, pk)
                cp("act", st30[0:3, :], pt[0:3, :], R=[pk], W=["st30"])
                dsts_ = O["sconv_p"][l, sq] if kind == "p" else O["sconv_s"][l, 0]
                dma_sp(dsts_[:, g * 512:(g + 1) * 512], st30[0:3, :], R=["st30"], W=[("o_sconv", l, sq, g)])
            else:
                cp("dve", xhist[:, g * 4:g * 4 + 4, :], rb_[:, :, T:T + 3], R=[rk_], W=["xhist"])
            for c4 in range(4):
                c = g * 4 + c4
                pt, pk = PS()
                for k in range(4):
                    mm(pt[:, 0:T], dvx[:, (c4 * 4 + k), :], rb_[:, c4, k:k + T], k == 0, k == 3, R=[wkd, rk_], W=[pk])
                ob, ok = bpool.get()
                act(ob[:, 0:T], pt[:, 0:T], AF.Silu, R=[pk, "params"], W=[ok], bias=pcol("ssm_conv_b", l, c))
                xact[c] = (ob, ok)
        if kind == 's' and cfg.stop <= 7:
            raise _Stop()
        ws, wk = WS.next("dt")
        wdt = ws[:, 0:KC * SH].rearrange("p (kc n) -> p kc n", n=SH)
        dts = []
        for b in range(NB):
            nb = min(128, T - b * 128)
            pt, pk = PS()
            for kc in range(KC):
                mm(pt[0:nb, 0:SH], hT[kc][0][:, b * 128:b * 128 + nb], wdt[:, kc, :], kc == 0, kc == KC - 1,
                   R=[wk, hk[kc]], W=[pk])
            db, dk = fpool.get()
            tt("dve", db[0:nb, 0:SH], pt[0:nb, 0:SH], bcp[0:nb, 0, l * SH:(l + 1) * SH], ALU.add, R=[pk, "bcp"], W=[dk])
            act(db[0:nb, 0:SH], db[0:nb, 0:SH], AF.Exp, R=[dk], W=[dk])
            act(db[0:nb, 0:SH], db[0:nb, 0:SH], AF.Ln, R=[dk], W=[dk], bias=1.0)
            dts.append((db, dk))

        if kind == 's' and cfg.stop <= 8:
            raise _Stop()
        for b in range(NB):
            nb = min(128, T - b * 128)
            t0 = b * 128
            db, dk = dts[b]
            dt_ap = db[0:nb, 0:SH]
            da = sm[0:nb, 1, :]
            tt("dve", da, dt_ap, bcp[0:nb, 1, l * SH:(l + 1) * SH], ALU.mult, R=[dk, "bcp"], W=["sm_da"])
            pcs, pkcs = PS()
            mm(pcs[0:nb, 0:SH], tri_f[0:nb, 0:nb], da, True, True, R=["tri_f", "sm_da"], W=[pkcs])
            mm(pcs[:, SH:2 * SH], ones_f[0:nb, :], da, True, True, R=["ones_f", "sm_da"], W=[pkcs])
            cs = sm[0:nb, 2, :]
            cp("dve", cs, pcs[0:nb, 0:SH], R=[pkcs], W=["sm_cs"])
            ecs = sm[0:nb, 3, :]
            act(ecs, pcs[0:nb, 0:SH], AF.Exp, R=[pkcs], W=["sm_ecs"])
            dte = sm[0:nb, 4, :]
            tt("dve", dte, pcs[0:nb, SH:2 * SH], cs, ALU.subtract, R=[pkcs, "sm_cs"], W=["sm_dte"])
            act(dte, dte, AF.Exp, R=["sm_dte"], W=["sm_dte"])
            etot = sm[:, 5, :]
            act(etot, pcs[:, SH:2 * SH], AF.Exp, R=[pkcs], W=["sm_etot"])
            pxs, pkxs = PSB()
            for c in range(8):
                tr(pxs[0:nb, c * 128:(c + 1) * 128], xact[c][0][:, t0:t0 + nb], ident_b[:], R=[xact[c][1], "ident_b"], W=[pkxs])
            xs3 = pxs[0:nb, :].rearrange("p (h d) -> p h d", d=HP)
            tt("dve", xdt_t[0:nb], xs3, bc_ap(db, nb, [[1, SH], [0, HP]]), ALU.mult, R=[pkxs, dk], W=["xdt"])
            tt("dve", xD_t[0:nb, :].rearrange("p (h d) -> p h d", d=HP), xs3,
               bc_ap(bcp, nb, [[1, SH], [0, HP]], offset=2 * L * SH + l * SH), ALU.mult, R=[pkxs, "bcp"], W=["xD"])
            tt("dve", xdte_t[0:nb], xdt_t[0:nb], bc_ap(sm, nb, [[1, SH], [0, HP]], offset=4 * SH), ALU.mult,
               R=["xdt", "sm_dte"], W=["xdte"])
            pbt, pkbt = PSB()
            for g in range(SG):
                tr(pbt[0:nb, g * 128:(g + 1) * 128], xact[8 + g][0][:, t0:t0 + nb], ident_b[:],
                   R=[xact[8 + g][1], "ident_b"], W=[pkbt])
            cp("act", Btm_t[0:nb, :], pbt[0:nb, 0:512], R=[pkbt], W=["Btm"])
            tt("dve", R_t[0:nb, :, 0:nb], bc_ap(tri_b, nb, [[0, SH], [1, nb]]),
               bc_ap(sm, nb, [[1, SH], [0, nb]], offset=1 * SH), ALU.mult, R=["tri_b", "sm_da"], W=["R_t"])
            hpb = min(SH, max(1, 512 // nb))
            for h0 in range(0, SH, hpb):
                pt, pk = PS()
                mm(pt[0:nb, 0:hpb * nb], lst_b[0:nb, 0:nb], R_t[0:nb, h0:h0 + hpb, 0:nb], True, True,
                   R=["lst_b", "R_t"], W=[pk])
                act(eD_t[0:nb, h0:h0 + hpb, 0:nb], pt[0:nb, 0:hpb * nb].rearrange("p (h l) -> p h l", l=nb), AF.Exp,
                    R=[pk], W=["eD_t"])
            pg, pkg_ = PS()
            for g in range(SG):
                mm(pg[0:nb, g * nb:(g + 1) * nb], xact[8 + g][0][:, t0:t0 + nb], xact[12 + g][0][:, t0:t0 + nb], True, True,
                   R=[xact[8 + g][1], xact[12 + g][1]], W=[pkg_])
            tt("dve", Gm_t[0:nb, :, 0:nb], pg[0:nb, 0:SG * nb].rearrange("p (g l) -> p g l", l=nb),
               bc_ap(tri_b, nb, [[0, SG], [1, nb]]), ALU.mult, R=[pkg_, "tri_b"], W=["Gm_t"])
            for g in range(SG):
                tt("dve", eD_t[0:nb, g * 4:(g + 1) * 4, 0:nb], eD_t[0:nb, g * 4:(g + 1) * 4, 0:nb],
                   bc_ap(Gm_t, nb, [[0, 4], [1, nb]], offset=g * 128), ALU.mult, R=["eD_t", "Gm_t"], W=["eD_t"])
            pyd = [PS(), PS()]
            for h in range(SH):
                pt, pk = pyd[h // 8]
                mm(pt[0:nb, (h % 8) * HP:(h % 8 + 1) * HP], eD_t[0:nb, h, 0:nb], xdt_t[0:nb, h, :], True, True,
                   R=["eD_t", "xdt"], W=[pk])
            pyo = [PS(), PS()]
            for g in range(SG):
                pt, pk = pyo[g // 2]
                mm(pt[0:nb, (g % 2) * 256:(g % 2 + 1) * 256], xact[12 + g][0][:, t0:t0 + nb], state_b[:, g * 256:(g + 1) * 256],
                   True, True, R=[xact[12 + g][1], "state_b"], W=[pk])
            for half in range(2):
                ysl = yt_t[0:nb, half * 512:(half + 1) * 512]
                tt("dve", ysl.rearrange("p (h d) -> p h d", d=HP),
                   pyo[half][0][0:nb, :].rearrange("p (h d) -> p h d", d=HP),
                   bc_ap(sm, nb, [[1, 8], [0, HP]], offset=3 * SH + half * 8), ALU.mult,
                   R=[pyo[half][1], "sm_ecs"], W=["yt_t"])
                tt("dve", ysl, ysl, pyd[half][0][0:nb, :], ALU.add, R=["yt_t", pyd[half][1]], W=["yt_t"])
                tt("dve", ybf_t[0:nb, half * 512:(half + 1) * 512], ysl, xD_t[0:nb, half * 512:(half + 1) * 512], ALU.add,
                   R=["yt_t", "xD"], W=["ybf_t"])
            pyt, pkyt = PSB()
            for c in range(8):
                tr(pyt[:, c * 128:c * 128 + nb], ybf_t[0:nb, c * 128:(c + 1) * 128], ident_b[0:nb, 0:nb],
                   R=["ybf_t", "ident_b"], W=[pkyt])
            cp("act", yT[:, :, t0:t0 + nb], pyt[:, :].rearrange("p (c t) -> p c t", t=128)[:, :, 0:nb], R=[pkyt], W=["yT"])
            pst = [PS(), PS()]
            for g in range(SG):
                pt, pk = pst[g // 2]
                mm(pt[:, (g % 2) * 256:(g % 2 + 1) * 256], Btm_t[0:nb, g * 128:(g + 1) * 128],
                   xdte_t[0:nb, g * 4:(g + 1) * 4, :], True, True, R=["Btm", "xdte"], W=[pk])
            tt("dve", state_f[:, :].rearrange("p (h d) -> p h d", d=HP), state_f[:, :].rearrange("p (h d) -> p h d", d=HP),
               bc_ap(sm, 128, [[1, SH], [0, HP]], offset=5 * SH), ALU.mult, R=["state_f", "sm_etot"], W=["state_f"])
            for half in range(2):
                tt("dve", state_f[:, half * 512:(half + 1) * 512], state_f[:, half * 512:(half + 1) * 512],
                   pst[half][0][:, :], ALU.add, R=["state_f", pst[half][1]], W=["state_f"])
            cp("pool", state_b[:], state_f[:], R=["state_f"], W=["state_b"])
            fpool.put(dts[b])
        for c in range(16):
            bpool.put(xact[c])
        if last:
            for half in range(2):
                pt, pk = PS()
                for c4 in range(4):
                    c = half * 4 + c4
                    tr(pt[:, c4 * 128:(c4 + 1) * 128], state_f[:, c * 128:(c + 1) * 128], ident_f[:],
                       R=["state_f", "ident_f"], W=[pk])
                cp("act", tokout[:, half * 512:(half + 1) * 512], pt[:, :], R=[pk], W=["tokout"])
            dstt = O["ssm_p"][l, sq] if kind == "p" else O["ssm_s"][l, 0]
            dma_sp(dstt.rearrange("(c h) p n -> (h p) c n", h=2), tokout[:, :].rearrange("q (c n) -> q c n", n=128),
                   R=["tokout"], W=[("o_ssm", l, sq)])
        if kind == 's' and cfg.stop <= 9:
            raise _Stop()
        yz = []
        for g in range(2):
            ws, wk = WS.next("z%d" % g)
            for c4 in range(4):
                c = g * 4 + c4
                pt, pk = PS()
                win_chunk(ws, wk, KC, 512, c4 * 128, 128, pt[:, 0:T], pk)
                zb, zk = bpool.get()
                act(zb[:, 0:T], pt[:, 0:T], AF.Silu, R=[pk], W=[zk])
                tt("dve", zb[:, 0:T], zb[:, 0:T], yT[:, c, 0:T], ALU.mult, R=[zk, "yT"], W=[zk])
                yz.append((zb, zk))
        for g in range(SG):
            fb, fk = sumsq([(yz[2 * g + i][0][:, 0:T], yz[2 * g + i][1]) for i in range(2)], T, 256)
            for i in range(2):
                c = 2 * g + i
                stt("dve", yz[c][0][:, 0:T], yz[c][0][:, 0:T], pcol("g_ssm", l, c), fb[:, 0:T], ALU.mult, ALU.mult,
                    R=[yz[c][1], fk, "params"], W=[yz[c][1]])
            fpool.put((fb, fk))

        if kind == 's' and cfg.stop <= 10:
            raise _Stop()
        if kind == os.environ.get("DBGKIND", "s") and DBG:
            def dbg_dump(dst, src, np_, shp, R):
                n = 1
                for v in shp:
                    n *= v
                view = tokout[0:np_, 0:n]
                if len(shp) == 2:
                    view = view.rearrange("p (a b) -> p a b", b=shp[1])
                cp("dve", view, src, R=R, W=["tokout"])
                dma_sp(dst, view, R=["tokout"], W=["dbgout"])
            if "Kc0" in DBG:
                dbg_dump(DBG["Kc0"], Kc[:, 0, 0:1024], QD, [1024], [("Kc", 0)])
            if "Vc0" in DBG:
                dbg_dump(DBG["Vc0"], Vc[:, 0:8, 0, 0:64], 128, [8, 64], [("Vc", kb) for kb in range(8)])
            if "Qp0" in DBG:
                dbg_dump(DBG["Qp0"], Qp[:, 0, 0:T], QD, [T], [("Qp", 0)])
            for c in range(4):
                if "cT" in DBG:
                    dbg_dump(DBG["cT"][c], cT[c][0][:, 0:T], 128, [T], [cT[c][1]])
            if "On" in DBG:
                for hh_ in range(0, NH, 4):
                    dbg_dump(DBG["On"][:, hh_:hh_ + 4, :], On[:, hh_:hh_ + 4, 0:T], 64, [4, T], [("On", h) for h in range(NH)])
            for c in range(8):
                if "yz" in DBG:
                    dbg_dump(DBG["yz"][c], yz[c][0][:, 0:T], 128, [T], [yz[c][1]])
        merged = []
        for cb in range(2):
            macc = [fpool.get() for _ in range(4)]
            for i in range(3):
                ws, wk = WS.next("gate%d_%d" % (i, cb))
                gts = []
                gpairs = []
                for c2 in (0, 2):
                    pt, pk = PS()
                    win_chunk(ws, wk, KC, 512, c2 * 128, 128, pt[:, 0:T], pk)
                    win_chunk(ws, wk, KC, 512, (c2 + 1) * 128, 128, pt[:, TT:TT + T], pk)
                    gf, gk = fpool.get()
                    gv = gf[:].bitcast(BF16)
                    act(gv[:, 0:2 * TT].rearrange("p (a t) -> p a t", t=TT)[:, :, 0:T],
                        pt[:, 0:2 * TT].rearrange("p (a t) -> p a t", t=TT)[:, :, 0:T], AF.Sigmoid, R=[pk], W=[gk])
                    gts.append((gv[:, 0:TT], gk))
                    gts.append((gv[:, TT:2 * TT], gk))
                    gpairs.append((gf, gk))
                wsm, wkm = WS.next(("mconv%d", "mattn%d", "mssm%d")[i] % cb)
                for c4 in range(4):
                    cs_ = slice(c4 * 128, (c4 + 1) * 128)
                    pt, pk = PS()
                    if i == 0:
                        wvc = wsm[:, 0:4 * 512].rearrange("p (kc n) -> p kc n", n=512)
                        for kc in range(4):
                            mm(pt[:, 0:T], wvc[:, kc, cs_], cT[kc][0][:, 0:T], kc == 0, kc == 3, R=[wkm, cT[kc][1]], W=[pk])
                    elif i == 1:
                        wva = wsm[0:64, 0:NH * 512].rearrange("p (h n) -> p h n", n=512)
                        for h in range(NH):
                            mm(pt[:, 0:T], wva[:, h, cs_], On[:, h, 0:T], h == 0, h == NH - 1, R=[wkm, ("On", h)], W=[pk])
                    else:
                        wvs = wsm[:, 0:8 * 512].rearrange("p (kc n) -> p kc n", n=512)
                        for kc in range(8):
                            mm(pt[:, 0:T], wvs[:, kc, cs_], yz[kc][0][:, 0:T], kc == 0, kc == 7, R=[wkm, yz[kc][1]], W=[pk])
                    m0, mk0 = macc[c4]
                    if i == 0:
                        tt("dve", m0[:, 0:T], pt[:, 0:T], gts[c4][0][:, 0:T], ALU.mult, R=[pk, gts[c4][1]], W=[mk0])
                    else:
                        m1, mk1 = fpool.get()
                        tt("dve", m1[:, 0:T], pt[:, 0:T], gts[c4][0][:, 0:T], ALU.mult, R=[pk, gts[c4][1]], W=[mk1])
                        if i == 1:
                            tt("dve", m0[:, 0:T], m0[:, 0:T], m1[:, 0:T], ALU.add, R=[mk0, mk1], W=[mk0])
                        else:
                            mb_, mbk = bpool.get()
                            tt("dve", mb_[:, 0:T], m0[:, 0:T], m1[:, 0:T], ALU.add, R=[mk0, mk1], W=[mbk])
                            merged.append((mb_, mbk))
                        fpool.put((m1, mk1))
                    if c4 % 2 == 1:
                        fpool.put(gpairs[c4 // 2])
            for c4 in range(4):
                fpool.put(macc[c4])
        for c in range(4):
            bpool.put(cT[c])
        for c in range(8):
            bpool.put(yz[c])
        for c in range(KC):
            bpool.put(hT[c])

        def proj_norm_residual(chunks_fn, gname):
            outs = []
            for c in range(KC):
                pt, pk = chunks_fn(c)
                ob, ok = fpool.get()
                cp("act" if c % 2 else "dve", ob[:, 0:T], pt[:, 0:T], R=[pk], W=[ok])
                outs.append((ob, ok))
            fb, fk = sumsq([(o[0][:, 0:T], o[1]) for o in outs], T, D)
            for c in range(KC):
                ob, ok = outs[c]
                stt("dve", ob[:, 0:T], ob[:, 0:T], pcol(gname, l, c), fb[:, 0:T], ALU.mult, ALU.mult,
                    R=[ok, fk, "params"], W=[ok])
                tt("dve", xb[:, c, 0:T], xb[:, c, 0:T], ob[:, 0:T], ALU.add, R=[xk, ok], W=[xk])
                fpool.put((ob, ok))
            fpool.put((fb, fk))

        wo = {}

        def out_chunk(c):
            cb = c // 4
            if c % 4 == 0:
                wo["w"] = WS.next("wout%d" % cb)
            ws, wk = wo["w"]
            wv_ = ws[:, 0:KC * 512].rearrange("p (kc n) -> p kc n", n=512)
            pt, pk = PS()
            for kc in range(KC):
                mm(pt[:, 0:T], wv_[:, kc, (c % 4) * 128:(c % 4 + 1) * 128], merged[kc][0][:, 0:T], kc == 0, kc == KC - 1,
                   R=[wk, merged[kc][1]], W=[pk])
            return pt, pk
        proj_norm_residual(out_chunk, "g_post_mix")
        for c in range(KC):
            bpool.put(merged[c])

        fb, fk = sumsq([(xb[:, c, 0:T], xk) for c in range(KC)], T, D)
        h2 = [bpool.get() for _ in range(KC)]
        for c in range(KC):
            stt("dve", h2[c][0][:, 0:T], xb[:, c, 0:T], pcol("g_pre_ffn", l, c), fb[:, 0:T], ALU.mult, ALU.mult,
                R=[xk, fk, "params"], W=[h2[c][1]])
        fpool.put((fb, fk))
        actc = [None] * FC
        for g in range(11):
            ws, wk = WS.next("gu%d" % g)
            wv_ = ws[:, 0:KC * 512].rearrange("p (kc n) -> p kc n", n=512)
            for c4 in range(4):
                cc = g * 4 + c4
                pt, pk = PS()
                for kc in range(KC):
                    mm(pt[:, 0:T], wv_[:, kc, c4 * 128:(c4 + 1) * 128], h2[kc][0][:, 0:T], kc == 0, kc == KC - 1,
                       R=[wk, h2[kc][1]], W=[pk])
                if cc < FC:
                    ab, ak = bpool.get()
                    act(ab[:, 0:T], pt[:, 0:T], AF.Silu, R=[pk], W=[ak])
                    actc[cc] = (ab, ak)
                else:
                    ab, ak = actc[cc - FC]
                    tt("dve", ab[:, 0:T], pt[:, 0:T], ab[:, 0:T], ALU.mult, R=[pk, ak], W=[ak])
        for c in range(KC):
            bpool.put(h2[c])
        dn = {}

        def down_chunks(cb):
            pts = [PS() for _ in range(4)]
            for kg in range(3):
                nk = 8 if kg < 2 else 6
                ws, wk = WS.next("dn%d_%d" % (cb, kg))
                wv_ = ws[:, 0:nk * 512].rearrange("p (kc n) -> p kc n", n=512)
                for c4 in range(4):
                    for kc in range(nk):
                        kk = kg * 8 + kc
                        mm(pts[c4][0][:, 0:T], wv_[:, kc, c4 * 128:(c4 + 1) * 128], actc[kk][0][:, 0:T],
                           kk == 0, kk == FC - 1, R=[wk, actc[kk][1]], W=[pts[c4][1]])
            return pts

        def down_chunk(c):
            if c % 4 == 0:
                dn["p"] = down_chunks(c // 4)
            return dn["p"][c % 4]
        proj_norm_residual(down_chunk, "g_post_ffn")
        for c in range(FC):
            bpool.put(actc[c])

        if l == cfg.depth - 1:
            for b in range(NB):
                nb = min(128, T - b * 128)
                for half in range(2):
                    pt, pk = PS()
                    for c4 in range(4):
                        c = half * 4 + c4
                        tr(pt[0:nb, c4 * 128:(c4 + 1) * 128], xb[:, c, b * 128:b * 128 + nb], ident_f[:], R=[xk, "ident_f"], W=[pk])
                    cp("act", tokout[0:nb, half * 512:(half + 1) * 512], pt[0:nb, :], R=[pk], W=["tokout"])
                dst = (O["y_prompt"][sq, pos0 + b * 128:pos0 + b * 128 + nb, :] if kind == "p" else O["y_sample"][0, 0:nb, :])
                dma_sp(dst, tokout[0:nb, :], R=["tokout"], W=[("o_y", sq, j, b)])
        else:
            p0 = j * TT if kind == "p" else 0
            dma_sp(xres[sq, :, :, p0:p0 + T], xb[:, :, 0:T], R=[xk], W=[("xres", sq, j)])

    try:
        for ti in range(len(tiles)):
            do_tile(ti)
        assert WS.taken == len(WS.plan), (WS.taken, len(WS.plan))
    except _Stop:
        pass


_OUT_ORDER = ["y_prompt", "y_sample", "lat_p", "kr_p", "conv_p", "sconv_p", "ssm_p",
              "lat_s", "kr_s", "conv_s", "sconv_s", "ssm_s"]


def kernel(**inputs):
    n = 8
    cfg = Cfg()
    nc = build(cfg)
    consts = host_consts(cfg)
    arr = {k: np.ascontiguousarray(np.asarray(v, dtype=np.float32)) for k, v in inputs.items()}
    in_maps = []
    for c in range(n):
        m = {}
        m["x_prompt"] = np.ascontiguousarray(arr["x_prompt"][2 * c:2 * c + 2])
        m["x_sample"] = np.ascontiguousarray(arr["x_sample"][c:c + 1])
        for k in ("cache_mla_latent", "cache_mla_rope", "state_conv", "state_ssm_conv", "state_ssm"):
            m[k] = np.ascontiguousarray(arr[k][:, c:c + 1])
        for k in WNAMES:
            m[k] = arr[k]
        m["c_ident"] = consts["ident"]
        m["c_tri"] = consts["tri"]
        m["c_lstrict"] = consts["lstrict"]
        m["c_rcos"] = consts["rcos"]
        m["c_rsin"] = consts["rsin"]
        in_maps.append(m)
    res = run_bass_kernel_spmd(nc, in_maps, core_ids=list(range(n)))
    r = res.results
    outs = []
    for k in _OUT_ORDER:
        ax = 0 if k in ("y_prompt", "y_sample") else 1
        outs.append(np.concatenate([np.asarray(r[c][k], dtype=np.float32) for c in range(n)], axis=ax))
    return tuple(outs)
```
